# Optimizing a Trainium2 kernel written in Bass

```python
import jax, jax.numpy as jnp
from jax import lax
import numpy as np

D_MODEL = 1024
BATCH = 4
SEQ = 8192
DEPTH = 2

HEAD_DIM = 64
RMS_EPS = 1e-6
NEG_INF = -1e30
ATTN_SCALE = HEAD_DIM ** -0.5

POOL_WIDTH = 512
POOL_GROUPS = 4
POOL_GROUP_DIM = POOL_WIDTH // POOL_GROUPS
POOL_WINDOWS = (2, 4, 8, 16)

SWA_Q_HEADS = 8
SWA_KV_HEADS = 2
SWA_GROUP = SWA_Q_HEADS // SWA_KV_HEADS
SWA_WINDOW = 128
SWA_BLOCK = 128
SWA_WIDTH = SWA_Q_HEADS * HEAD_DIM
SWA_KV_WIDTH = SWA_KV_HEADS * HEAD_DIM

MOBA_HEADS = 8
MOBA_BLOCK = 256
MOBA_TOPK = 3
MOBA_Q_CHUNK = 64
MOBA_WIDTH = MOBA_HEADS * HEAD_DIM

N_BRANCH = 3
IN_SIZES = (POOL_WIDTH, POOL_WIDTH,
            SWA_WIDTH, SWA_KV_WIDTH, SWA_KV_WIDTH, SWA_WIDTH,
            MOBA_WIDTH, MOBA_WIDTH, MOBA_WIDTH, MOBA_WIDTH,
            N_BRANCH * D_MODEL)
IN_WIDTH = sum(IN_SIZES)

kernel_name = "hybrid_pool_swa_moba_gated_merge"


def _rmsnorm(x, g):
    x32 = x.astype(jnp.float32)
    y = x32 * lax.rsqrt(jnp.mean(x32 * x32, axis=-1, keepdims=True) + RMS_EPS)
    return (y * g.astype(jnp.float32)).astype(x.dtype)


def _pool_mixer(xa, w_pool, scale):
    b, s, _ = xa.shape
    x32 = xa.astype(jnp.float32)
    cs = jnp.cumsum(x32, axis=1)
    t = jnp.arange(s)
    outs = []
    for gi, w in enumerate(POOL_WINDOWS):
        cg = cs[..., gi * POOL_GROUP_DIM:(gi + 1) * POOL_GROUP_DIM]
        lag = jnp.pad(cg, ((0, 0), (w, 0), (0, 0)))[:, :s]
        cnt = jnp.minimum(t + 1, w).astype(jnp.float32)
        outs.append((cg - lag) / cnt[None, :, None])
    pooled = (jnp.concatenate(outs, axis=-1) - x32).astype(xa.dtype)
    pooled = pooled.reshape(b, s, POOL_GROUPS, POOL_GROUP_DIM)
    mixed = jnp.einsum('bsgc,gcd->bsgd', pooled, w_pool).reshape(b, s, POOL_WIDTH)
    return mixed * scale


def _swa_attention(q, k, v, sinks):
    b, s = q.shape[0], q.shape[1]
    nb = s // SWA_BLOCK
    qb = q.reshape(b, nb, SWA_BLOCK, SWA_KV_HEADS, SWA_GROUP, HEAD_DIM)
    kb = k.reshape(b, nb, SWA_BLOCK, SWA_KV_HEADS, HEAD_DIM)
    vb = v.reshape(b, nb, SWA_BLOCK, SWA_KV_HEADS, HEAD_DIM)

    def with_prev(t):
        prev = jnp.pad(t[:, :-1], ((0, 0), (1, 0), (0, 0), (0, 0), (0, 0)))
        return jnp.concatenate([prev, t], axis=2)

    kk = with_prev(kb)
    vv = with_prev(vb)
    sc = jnp.einsum('bnqhgd,bnkhd->bnhgqk', qb, kk,
                    preferred_element_type=jnp.float32) * ATTN_SCALE
    qi = jnp.arange(SWA_BLOCK)[:, None] + SWA_BLOCK
    kj = jnp.arange(2 * SWA_BLOCK)[None, :]
    diff = qi - kj
    band = (diff >= 0) & (diff < SWA_WINDOW)
    kpos = jnp.arange(nb)[:, None, None] * SWA_BLOCK + kj[None] - SWA_BLOCK
    valid = band[None] & (kpos >= 0)
    sc = jnp.where(valid[None, :, None, None], sc, NEG_INF)
    sink = sinks.astype(jnp.float32).reshape(1, 1, SWA_KV_HEADS, SWA_GROUP, 1)
    m = jnp.maximum(jnp.max(sc, axis=-1), sink)
    p = jnp.exp(sc - m[..., None])
    denom = jnp.sum(p, axis=-1) + jnp.exp(sink - m)
    p = (p / denom[..., None]).astype(v.dtype)
    o = jnp.einsum('bnhgqk,bnkhd->bnqhgd', p, vv)
    return o.reshape(b, s, SWA_WIDTH)


def _moba_attention(q, k, v):
    b, s = q.shape[0], q.shape[1]
    nblk = -(-s // MOBA_BLOCK)
    s_pad = nblk * MOBA_BLOCK
    pad = ((0, 0), (0, s_pad - s), (0, 0), (0, 0))
    q = jnp.pad(q, pad)
    k = jnp.pad(k, pad)
    v = jnp.pad(v, pad)
    topk = min(MOBA_TOPK, nblk)
    kb = k.reshape(b, nblk, MOBA_BLOCK, MOBA_HEADS, HEAD_DIM)
    vb = v.reshape(b, nblk, MOBA_BLOCK, MOBA_HEADS, HEAD_DIM)
    k_mean = jnp.mean(kb.astype(jnp.float32), axis=2)
    k_bh = kb.transpose(0, 3, 1, 2, 4)
    v_bh = vb.transpose(0, 3, 1, 2, 4)
    bi = jnp.arange(b)[:, None, None, None]
    hi = jnp.arange(MOBA_HEADS)[None, None, :, None]
    n_chunks = s_pad // MOBA_Q_CHUNK

    def chunk_fn(c):
        q0 = c * MOBA_Q_CHUNK
        blk = q0 // MOBA_BLOCK
        qpos = q0 + jnp.arange(MOBA_Q_CHUNK)
        qc = lax.dynamic_slice_in_dim(q, q0, MOBA_Q_CHUNK, axis=1)
        gate = jnp.einsum('bqhd,bnhd->bqhn', qc.astype(jnp.float32), k_mean)
        past = jnp.arange(nblk) < blk
        gate = jnp.where(past[None, None, None, :], gate, NEG_INF)
        _, idx = lax.top_k(gate, topk)
        slot_valid = jnp.arange(topk) < blk
        k_sel = k_bh[bi, hi, idx]
        v_sel = v_bh[bi, hi, idx]
        s_sel = jnp.einsum('bqhd,bqhjkd->bqhjk', qc, k_sel,
                           preferred_element_type=jnp.float32) * ATTN_SCALE
        s_sel = jnp.where(slot_valid[:, None], s_sel, NEG_INF)
        s_sel = s_sel.reshape(b, MOBA_Q_CHUNK, MOBA_HEADS, topk * MOBA_BLOCK)
        k_own = lax.dynamic_slice_in_dim(kb, blk, 1, axis=1)[:, 0]
        v_own = lax.dynamic_slice_in_dim(vb, blk, 1, axis=1)[:, 0]
        s_own = jnp.einsum('bqhd,bkhd->bqhk', qc, k_own,
                           preferred_element_type=jnp.float32) * ATTN_SCALE
        kpos = blk * MOBA_BLOCK + jnp.arange(MOBA_BLOCK)
        causal = kpos[None, :] <= qpos[:, None]
        s_own = jnp.where(causal[None, :, None, :], s_own, NEG_INF)
        p = jax.nn.softmax(jnp.concatenate([s_sel, s_own], axis=-1), axis=-1).astype(v.dtype)
        p_sel = p[..., :topk * MOBA_BLOCK].reshape(b, MOBA_Q_CHUNK, MOBA_HEADS, topk, MOBA_BLOCK)
        p_own = p[..., topk * MOBA_BLOCK:]
        o = (jnp.einsum('bqhjk,bqhjkd->bqhd', p_sel, v_sel)
             + jnp.einsum('bqhk,bkhd->bqhd', p_own, v_own))
        return o.astype(q.dtype)

    outs = lax.map(chunk_fn, jnp.arange(n_chunks))
    o = outs.transpose(1, 0, 2, 3, 4).reshape(b, s_pad, MOBA_WIDTH)
    return o[:, :s]


def _layer(x, norm_g, w_in, pool_w, pool_scale, sinks, w_proj_a, w_proj_b, w_proj_c, w_out):
    b, s, _ = x.shape
    h = _rmsnorm(x, norm_g)
    u = h @ w_in
    offsets = np.cumsum(IN_SIZES)[:-1].tolist()
    (a_x, a_z, b_q, b_k, b_v, b_z, c_q, c_k, c_v, c_z, gate_logits) = jnp.split(u, offsets, axis=-1)
    ya = _pool_mixer(a_x, pool_w, pool_scale) * jax.nn.silu(a_z)
    yb = _swa_attention(b_q.reshape(b, s, SWA_Q_HEADS, HEAD_DIM),
                        b_k.reshape(b, s, SWA_KV_HEADS, HEAD_DIM),
                        b_v.reshape(b, s, SWA_KV_HEADS, HEAD_DIM), sinks) * jax.nn.silu(b_z)
    yc = _moba_attention(c_q.reshape(b, s, MOBA_HEADS, HEAD_DIM),
                         c_k.reshape(b, s, MOBA_HEADS, HEAD_DIM),
                         c_v.reshape(b, s, MOBA_HEADS, HEAD_DIM)) * jax.nn.silu(c_z)
    ga, gb, gc = jnp.split(jax.nn.sigmoid(gate_logits), N_BRANCH, axis=-1)
    merged = ga * (ya @ w_proj_a) + gb * (yb @ w_proj_b) + gc * (yc @ w_proj_c)
    return x + merged @ w_out


def setup_inputs(seed: int = 0) -> dict:
    key = jax.random.key(seed)
    ks = jax.random.split(key, 11)
    nrm = jax.random.normal
    x = nrm(ks[0], (BATCH, SEQ, D_MODEL), jnp.float32)
    norm_g = 1.0 + 0.05 * nrm(ks[1], (DEPTH, D_MODEL), jnp.float32)
    w_in = nrm(ks[2], (DEPTH, D_MODEL, IN_WIDTH), jnp.float32) * D_MODEL ** -0.5
    pool_w = nrm(ks[3], (DEPTH, POOL_GROUPS, POOL_GROUP_DIM, POOL_GROUP_DIM), jnp.float32) * POOL_GROUP_DIM ** -0.5
    pool_scale = 1.0 + 0.1 * nrm(ks[4], (DEPTH, POOL_WIDTH), jnp.float32)
    sink_logits = 0.5 * nrm(ks[5], (DEPTH, SWA_Q_HEADS), jnp.float32)
    w_proj_a = nrm(ks[6], (DEPTH, POOL_WIDTH, D_MODEL), jnp.float32) * POOL_WIDTH ** -0.5
    w_proj_b = nrm(ks[7], (DEPTH, SWA_WIDTH, D_MODEL), jnp.float32) * SWA_WIDTH ** -0.5
    w_proj_c = nrm(ks[8], (DEPTH, MOBA_WIDTH, D_MODEL), jnp.float32) * MOBA_WIDTH ** -0.5
    w_out = nrm(ks[9], (DEPTH, D_MODEL, D_MODEL), jnp.float32) * D_MODEL ** -0.5
    final_norm_g = 1.0 + 0.05 * nrm(ks[10], (D_MODEL,), jnp.float32)
    return {"x": x, "norm_g": norm_g, "w_in": w_in, "pool_w": pool_w, "pool_scale": pool_scale,
            "sink_logits": sink_logits, "w_proj_a": w_proj_a, "w_proj_b": w_proj_b,
            "w_proj_c": w_proj_c, "w_out": w_out, "final_norm_g": final_norm_g}


def reference(x, norm_g, w_in, pool_w, pool_scale, sink_logits, w_proj_a, w_proj_b, w_proj_c,
              w_out, final_norm_g):
    for l in range(DEPTH):
        x = _layer(x, norm_g[l], w_in[l], pool_w[l], pool_scale[l], sink_logits[l],
                   w_proj_a[l], w_proj_b[l], w_proj_c[l], w_out[l])
    return _rmsnorm(x, final_norm_g)
```

```python
import numpy as np
import ml_dtypes
from contextlib import ExitStack
import concourse.bass as bass
import concourse.mybir as mybir
from concourse.bass_utils import run_bass_kernel_spmd

F32 = mybir.dt.float32
BF16 = mybir.dt.bfloat16
AF = mybir.ActivationFunctionType
ALU = mybir.AluOpType

D = 1024
TB = 256
NU = 22
UE = 4096
NEG = -30000.0
NWS = 5
NKV = 3
EPS = 1e-6

CF_ID = 0
CF_BAND = 128
CF_E64 = CF_BAND + 12 * 128
CF_ONES = CF_E64 + 65
CF_S65 = CF_ONES + 64
CF_N = CF_S65 + 64
CB_ID = 0
CB_CAUS = 128
CB_OWN4 = CB_CAUS + 512
CB_PRV4 = CB_OWN4 + 512
CB_OH = CB_PRV4 + 512
CB_N = CB_OH + 32 * 128


class Buf:
    __slots__ = ("name", "w", "r")

    def __init__(self, name):
        self.name = name
        self.w = None
        self.r = []


class Trk:
    def __init__(self, nc):
        self.nc = nc
        self.engs = {"pe": nc.tensor, "act": nc.scalar, "dve": nc.vector, "pool": nc.gpsimd, "sp": nc.sync}
        self.sems = {}
        self.cnt = {}
        self.waited = {}
        self._stack = []
        for name in self.engs:
            self._newsem("E_" + name)

    def _newsem(self, key):
        cm = self.nc.semaphore(key)
        h = cm.__enter__()
        self._stack.append(cm)
        self.sems[key] = h
        self.cnt[key] = 0
        return h

    def close(self):
        for cm in reversed(self._stack):
            cm.__exit__(None, None, None)

    def _wait(self, engname, tok):
        if tok is None:
            return
        key, val = tok
        if self.waited.get((engname, key), 0) >= val:
            return
        self.engs[engname].wait_ge(self.sems[key], val)
        self.waited[(engname, key)] = val

    def _deps(self, engname, reads, writes):
        best = {}

        def add(t):
            if t is None:
                return
            if engname == "pe" and t[0] == "E_pe":
                return
            if best.get(t[0], 0) < t[1]:
                best[t[0]] = t[1]
        for b in reads:
            add(b.w)
        for b in writes:
            add(b.w)
            for t in b.r:
                add(t)
        for k, v in best.items():
            self._wait(engname, (k, v))

    def _commit(self, tok, reads, writes):
        for b in writes:
            b.w = tok
            b.r = []
        for b in reads:
            b.r.append(tok)
            if len(b.r) > 24:
                m = {}
                for t in b.r:
                    if m.get(t[0], 0) < t[1]:
                        m[t[0]] = t[1]
                b.r = list(m.items())

    def op(self, engname, fn, reads=(), writes=()):
        self._deps(engname, reads, writes)
        ins = fn()
        key = "E_" + engname
        ins.then_inc(self.sems[key], 1)
        self.cnt[key] += 1
        tok = (key, self.cnt[key])
        self._commit(tok, reads, writes)
        return tok

    def dma(self, qname, semkey, out, in_, reads=(), writes=()):
        if semkey not in self.sems:
            self._newsem(semkey)
        self._deps(qname, reads, writes)
        ins = self.engs[qname].dma_start(out=out, in_=in_)
        ins.then_inc(self.sems[semkey], 16)
        self.cnt[semkey] += 16
        tok = (semkey, self.cnt[semkey])
        self._commit(tok, reads, writes)
        return tok

    def barrier(self):
        toks = [(k, v) for k, v in self.cnt.items() if v > 0]
        for e in self.engs:
            for t in toks:
                self._wait(e, t)


class _Stop(Exception):
    pass


def build_nc(S, NL, stop=None):
    import os
    stop = stop or os.environ.get("MK_STOP")

    _hits = {}

    def stage(name):
        _hits[name] = _hits.get(name, 0) + 1
        if stop and stop.split("#")[0] == name and _hits[name] == int((stop + "#1").split("#")[1]):
            raise _Stop()
    NB = S // TB
    NT = S // 128
    nc = bass.Bass("TRN2", target_bir_lowering=False)
    x_in = nc.dram_tensor("x", [S, D], F32, kind="ExternalInput").ap()
    wall = nc.dram_tensor("wall", [NL * NU, 128, UE], F32, kind="ExternalInput").ap()
    gbc_in = nc.dram_tensor("gbc", [NL + 1, 128, D], F32, kind="ExternalInput").ap()
    poolw_in = nc.dram_tensor("poolw", [NL, 128, 512], F32, kind="ExternalInput").ap()
    pscale_in = nc.dram_tensor("pscale", [NL, 128, 4], F32, kind="ExternalInput").ap()
    sink_in = nc.dram_tensor("sink", [NL, 1, 1024], F32, kind="ExternalInput").ap()
    cstf_in = nc.dram_tensor("cstf", [128, CF_N], F32, kind="ExternalInput").ap()
    cstb_in = nc.dram_tensor("cstb", [128, CB_N], BF16, kind="ExternalInput").ap()
    out = nc.dram_tensor("out", [S, D], F32, kind="ExternalOutput").ap()
    wsc = nc.dram_tensor("wsc", [NL * NU, 128, UE], BF16, kind="Internal").ap()
    x1 = nc.dram_tensor("x1s", [S, D], F32, kind="Internal").ap()
    kcs = nc.dram_tensor("kcs", [NL * NB, 64, 2048], BF16, kind="Internal").ap()
    vcs = nc.dram_tensor("vcs", [NL * NB, 128, 1040], BF16, kind="Internal").ap()

    T = Trk(nc)
    PE = lambda fn, r=(), w=(): T.op("pe", fn, r, w)
    ACT = lambda fn, r=(), w=(): T.op("act", fn, r, w)
    DVE = lambda fn, r=(), w=(): T.op("dve", fn, r, w)
    POOL = lambda fn, r=(), w=(): T.op("pool", fn, r, w)

    wscB = [Buf("wsc%d" % u) for u in range(NL * NU)]
    with ExitStack() as es:
        stf = [es.enter_context(nc.sbuf_tensor("stf%d" % k, [128, UE], F32)) for k in range(2)]
        stb = [es.enter_context(nc.sbuf_tensor("stb%d" % k, [128, UE], BF16)) for k in range(2)]
        stfB = [Buf("stf%d" % k) for k in range(2)]
        stbB = [[Buf("stb%da" % k), Buf("stb%db" % k)] for k in range(2)]
        for u in range(NL * NU):
            k = u % 2
            T.dma("sp", "pl%d" % k, stf[k][:], wall[u], writes=[stfB[k]])
            h = UE // 2
            DVE(lambda: nc.vector.tensor_copy(out=stb[k][:, 0:h], in_=stf[k][:, 0:h]), [stfB[k]], [stbB[k][0]])
            ACT(lambda: nc.scalar.copy(out=stb[k][:, h:UE], in_=stf[k][:, h:UE]), [stfB[k]], [stbB[k][1]])
            T.dma("pool", "ps%d" % k, wsc[u], stb[k][:], reads=stbB[k], writes=[wscB[u]])
        T.barrier()
    if stop == "prologue":
        T.close()
        return nc

    with ExitStack() as es:
        def sb(name, shape, dt=F32):
            return es.enter_context(nc.sbuf_tensor("s_" + name, shape, dt))

        def psum(name):
            return es.enter_context(nc.psum_tensor("p_" + name, [128, 512], F32))

        cstf = sb("cstf", [128, CF_N]); cstfB = Buf("cstf")
        cstb = sb("cstb", [128, CB_N], BF16); cstbB = Buf("cstb")
        identf = cstf[:, CF_ID:CF_ID + 128]
        identb = cstb[:, CB_ID:CB_ID + 128]
        gbc = sb("gbc", [128, D]); gbcB = Buf("gbc")
        fgbc = sb("fgbc", [128, D]); fgbcB = Buf("fgbc")
        pwf = sb("pwf", [128, 512]); pwfB = Buf("pwf")
        pw = sb("pw", [128, 512], BF16); pwB = Buf("pw")
        pscale = sb("pscale", [128, 4]); pscB = Buf("pscale")
        esink = sb("esink", [1, 1024]); esinkB = Buf("esink")
        xt = [sb("xt%d" % k, [128, D]) for k in range(4)]; xtB = [Buf("xt%d" % k) for k in range(4)]
        xs = sb("xs", [128, D], BF16); xsB = Buf("xs")
        sqj = sb("sqj", [128, D], BF16); sqjB = Buf("sqj")
        ss2 = sb("ss2", [128, 2]); ss2B = Buf("ss2")
        rs2 = sb("rs2", [128, 2]); rs2B = Buf("rs2")
        hT = sb("hT", [128, 8 * TB], BF16); hTB = Buf("hT")
        hT3 = hT[:].rearrange("p (k t) -> p k t", k=8)
        ax = sb("ax", [128, 4 * 512]); axB = [Buf("ax%d" % k) for k in range(4)]
        thz = [sb("thz%d" % k, [128, 256]) for k in range(2)]; thzB = [Buf("thz%d" % k) for k in range(2)]
        sz2a = sb("sz2a", [128, 4 * 256], BF16); sz2aB = Buf("sz2a")
        sz2b = sb("sz2b", [64, 8 * 256], BF16); sz2bB = Buf("sz2b")
        sz2c = sb("sz2c", [64, 8 * 256], BF16); sz2cB = Buf("sz2c")
        thg = sb("thg", [128, 8 * 256], BF16); thgB = Buf("thg")
        pooledT = sb("pooledT", [128, 512], BF16); pooledTB = Buf("pooledT")
        yaT = sb("yaT", [128, 4 * 256], BF16); yaTB = Buf("yaT")
        QTb = sb("QTb", [64, 8 * 256], BF16); QTbB = Buf("QTb")
        KTb = sb("KTb", [64, 2 * 4 * 128], BF16); KTbB = [Buf("KTb%d" % k) for k in range(4)]
        Vb = sb("Vb", [128, 4 * 130], BF16); VbB = [Buf("Vb%d" % k) for k in range(4)]
        ybT = sb("ybT", [64, 8 * 256], BF16); ybTB = Buf("ybT")
        QTc = sb("QTc", [64, 8 * 256], BF16); QTcB = Buf("QTc")
        KTcur = sb("KTcur", [64, 2048], BF16); KTcurB = Buf("KTcur")
        Vcur = sb("Vcur", [128, 1040], BF16); VcurB = Buf("Vcur")
        kvK = [sb("kvK%d" % k, [64, 2048], BF16) for k in range(NKV)]
        kvV = [sb("kvV%d" % k, [128, 1040], BF16) for k in range(NKV)]
        kvB = [Buf("kv%d" % k) for k in range(NKV)]
        kmT = sb("kmT", [64, 8 * 32], BF16); kmTB = Buf("kmT")
        ksum = sb("ksum", [64, 8]); ksumB = Buf("ksum")
        pmask = sb("pmask", [128, 256]); pmaskB = Buf("pmask")
        gm = sb("gm", [128, 256]); gmB = Buf("gm")
        top8 = sb("top8", [128, 64]); top8B = Buf("top8")
        thr = sb("thr", [128, 8]); thrB = Buf("thr")
        selm = sb("selm", [128, 256], BF16); selmB = Buf("selm")
        selbT = sb("selbT", [32, 8 * 256], BF16); selbTB = Buf("selbT")
        PT = [sb("PT%d" % k, [128, 512], BF16) for k in range(3)]; PTB = [Buf("PT%d" % k) for k in range(3)]
        OTs = [sb("OTs%d" % k, [65, 512]) for k in range(2)]; OTsB = [Buf("OTs%d" % k) for k in range(2)]
        rec = sb("rec", [64, 512]); recB = Buf("rec")
        tmpn = sb("tmpn", [64, 512]); tmpnB = Buf("tmpn")
        ycT = sb("ycT", [64, 8 * 256], BF16); ycTB = Buf("ycT")
        acc = sb("acc", [128, 8 * 256]); accB = Buf("acc")
        tmpg = [sb("tmpg%d" % k, [128, 256]) for k in range(2)]; tmpgB = [Buf("tmpg%d" % k) for k in range(2)]
        mT = sb("mT", [128, 8 * 256], BF16); mTB = Buf("mT")
        mT3 = mT[:].rearrange("p (k t) -> p k t", k=8)
        ws = [sb("ws%d" % k, [128, UE], BF16) for k in range(NWS)]; wsB = [Buf("ws%d" % k) for k in range(NWS)]
        PS = [psum("PS%d" % k) for k in range(2)]; PSB = [Buf("PS%d" % k) for k in range(2)]
        PO = [psum("PO%d" % k) for k in range(4)]; POB = [Buf("PO%d" % k) for k in range(4)]
        G = [psum("G%d" % k) for k in range(2)]; GB = [Buf("G%d" % k) for k in range(2)]
        x1B = [Buf("x1_%d" % k) for k in range(NT)]
        kvsB = [Buf("kvs%d" % k) for k in range(NL * NB)]
        outB = Buf("out")

        gst = {"n": 0, "wide": True}
        GBK = [(G[0], GB[0]), (G[1], GB[1]), (PO[2], POB[2]), (PO[3], POB[3])]

        def galloc():
            nb = 4 if gst["wide"] else 2
            k = gst["n"] % nb
            gst["n"] += 1
            t, b = GBK[k]
            return t[:, 0:512], [b]

        def ghalf():
            pa, pb = galloc()
            return pa[:, 0:256], pb

        gfull = galloc

        wst = {"next": 0}
        total_units = NL * NB * NU

        def w_issue(gidx):
            l = gidx // (NB * NU)
            u = gidx % NU
            slot = gidx % NWS
            src = wsc[l * NU + u]
            if u in (10, 11, 18, 19):
                T.dma("sp", "wl%d" % slot, ws[slot][0:64, :], src[0:64, :], reads=[wscB[l * NU + u]], writes=[wsB[slot]])
            else:
                T.dma("sp", "wl%d" % slot, ws[slot][:], src, reads=[wscB[l * NU + u]], writes=[wsB[slot]])

        def w_get(l, i, u):
            gidx = (l * NB + i) * NU + u
            while wst["next"] <= min(gidx + NWS - 2, total_units - 1):
                w_issue(wst["next"])
                wst["next"] += 1
            slot = gidx % NWS
            return ws[slot], wsB[slot]

        T.dma("pool", "cst0", cstf[:], cstf_in, writes=[cstfB])
        T.dma("pool", "cst1", cstb[:], cstb_in, writes=[cstbB])
        T.dma("pool", "cst2", fgbc[:], gbc_in[NL], writes=[fgbcB])
        POOL(lambda: nc.gpsimd.memset(kmT[:], 0.0), [], [kmTB])
        POOL(lambda: nc.gpsimd.memset(Vcur[:], 1.0), [], [VcurB])
        POOL(lambda: nc.gpsimd.memset(Vb[:], 1.0), [], VbB)

        def xslot(i, tl):
            return (i % 2) * 2 + tl

        def load_x(l, i):
            for tl in range(2):
                t = i * 2 + tl
                s = xslot(i, tl)
                if l == 0:
                    T.dma("pool", "xl%d" % s, xt[s][:], x_in[t * 128:(t + 1) * 128, :], writes=[xtB[s]])
                else:
                    T.dma("pool", "xl%d" % s, xt[s][:], x1[t * 128:(t + 1) * 128, :], reads=[x1B[t]], writes=[xtB[s]])

        def layer_params(l):
            T.dma("pool", "lp0", gbc[:], gbc_in[l], writes=[gbcB])
            T.dma("pool", "lp1", pwf[:], poolw_in[l], writes=[pwfB])
            T.dma("pool", "lp2", pscale[:], pscale_in[l], writes=[pscB])
            T.dma("pool", "lp3", esink[:], sink_in[l], writes=[esinkB])
            DVE(lambda: nc.vector.tensor_copy(out=pw[:], in_=pwf[:]), [pwfB], [pwB])
            DVE(lambda: nc.vector.tensor_scalar(out=pscale[:], in0=pscale[:], scalar1=0.5, scalar2=None, op0=ALU.mult), [pscB], [pscB])
            ACT(lambda: nc.scalar.activation(out=esink[:], in_=esink[:], func=AF.Exp), [esinkB], [esinkB])
            POOL(lambda: nc.gpsimd.memset(pmask[:], -1e30), [], [pmaskB])

        def fm_run(specs):
            for p0 in range(0, len(specs), 2):
                pair = specs[p0:p0 + 2]
                bank, pb = galloc()
                for idx, (wslot, wbuf, cbase, j0, jn, evac) in enumerate(pair):
                    pa = bank[:, idx * 256:(idx + 1) * 256]
                    for ko in range(8):
                        c0 = cbase + ko * 128 + j0
                        PE(lambda: nc.tensor.matmul(pa[0:jn, :], lhsT=wslot[:, c0:c0 + jn], rhs=hT3[:, ko, :],
                                                    start=(ko == 0), stop=(ko == 7)), [wbuf, hTB], pb)
                for idx, (wslot, wbuf, cbase, j0, jn, evac) in enumerate(pair):
                    evac(bank[:, idx * 256:(idx + 1) * 256], pb)

        def fm_chunk(wslot, wbuf, cbase, j0, jn, evac):
            fm_run([(wslot, wbuf, cbase, j0, jn, evac)])

        def silu2_evac(dst_ap, dstB, np_):
            st = {"k": 0}

            def ev(pa, pb):
                k = st["k"] % 2
                st["k"] += 1
                ACT(lambda: nc.scalar.activation(out=thz[k][0:np_, :], in_=pa[0:np_, :], func=AF.Tanh, scale=0.5), pb, [thzB[k]])
                DVE(lambda: nc.vector.scalar_tensor_tensor(out=dst_ap, in0=thz[k][0:np_, :], scalar=1.0, in1=pa[0:np_, :],
                                                           op0=ALU.add, op1=ALU.mult), [thzB[k]] + pb, [dstB])
            return ev

        def gate_chunks(l, i, u0):
            for uu in range(2):
                wslot, wbuf = w_get(l, i, u0 + uu)
                specs = []
                for cc in range(4):
                    c = uu * 4 + cc

                    def ev(pa, pb, c=c):
                        ACT(lambda: nc.scalar.activation(out=thg[:, c * 256:(c + 1) * 256], in_=pa, func=AF.Tanh, scale=0.5), pb, [thgB])
                    specs.append((wslot, wbuf, cc * 1024, 0, 128, ev))
                fm_run(specs)

        def proj_merge(first, nk, lhs_of, rhs_of, rbufs, wbufs_of):
            for dc in range(8):
                pa, pb = ghalf()
                for k in range(nk):
                    PE(lambda: nc.tensor.matmul(pa, lhsT=lhs_of(k, dc), rhs=rhs_of(k), start=(k == 0), stop=(k == nk - 1)),
                       rbufs + [wbufs_of(k)], pb)
                a_sl = acc[:, dc * 256:(dc + 1) * 256]
                g_sl = thg[:, dc * 256:(dc + 1) * 256]
                if first:
                    DVE(lambda: nc.vector.scalar_tensor_tensor(out=a_sl, in0=g_sl, scalar=1.0, in1=pa, op0=ALU.add, op1=ALU.mult),
                        [thgB] + pb, [accB])
                else:
                    k2 = dc % 2
                    DVE(lambda: nc.vector.scalar_tensor_tensor(out=tmpg[k2][:], in0=g_sl, scalar=1.0, in1=pa, op0=ALU.add, op1=ALU.mult),
                        [thgB] + pb, [tmpgB[k2]])
                    DVE(lambda: nc.vector.tensor_tensor(out=a_sl, in0=a_sl, in1=tmpg[k2][:], op=ALU.add), [accB, tmpgB[k2]], [accB])

        def normalize(po_ap, pobufs, nq, dst3, dstB, sz3, szB, k2):
            n = nq
            DVE(lambda: nc.vector.tensor_copy(out=OTs[k2][:, 0:n], in_=po_ap), pobufs, [OTsB[k2]])
            pg, pgb = gfull()
            PE(lambda: nc.tensor.matmul(pg[0:64, 0:n], lhsT=cstf[0:65, CF_S65:CF_S65 + 64], rhs=OTs[k2][0:65, 0:n],
                                        start=True, stop=True), [OTsB[k2], cstfB], pgb)
            DVE(lambda: nc.vector.reciprocal(out=rec[:, 0:n], in_=pg[0:64, 0:n]), pgb, [recB])
            DVE(lambda: nc.vector.scalar_tensor_tensor(out=tmpn[:, 0:n], in0=OTs[k2][0:64, 0:n], scalar=0.5, in1=rec[:, 0:n],
                                                       op0=ALU.mult, op1=ALU.mult), [OTsB[k2], recB], [tmpnB])
            a = dst3.shape[1]
            DVE(lambda: nc.vector.tensor_tensor(out=dst3, in0=tmpn[:, 0:n].rearrange("p (a b) -> p a b", a=a), in1=sz3, op=ALU.mult),
                [tmpnB, szB], [dstB])

        try:
          for l in range(NL):
              stage("init")
              layer_params(l)
              load_x(l, 0)
              kv_state = {}
              for i in range(NB):
                  if i + 1 < NB:
                      load_x(l, i + 1)
                  stage("norm")
                  for tl in range(2):
                      s = xslot(i, tl)
                      ACT(lambda: nc.scalar.activation(out=sqj[:], in_=xt[s][:], func=AF.Square, accum_out=ss2[:, tl:tl + 1]),
                          [xtB[s]], [sqjB, ss2B])
                  DVE(lambda: nc.vector.tensor_scalar(out=rs2[:], in0=ss2[:], scalar1=1.0 / D, scalar2=EPS, op0=ALU.mult, op1=ALU.add),
                      [ss2B], [rs2B])
                  ACT(lambda: nc.scalar.activation(out=rs2[:], in_=rs2[:], func=AF.Sqrt), [rs2B], [rs2B])
                  DVE(lambda: nc.vector.reciprocal(out=rs2[:], in_=rs2[:]), [rs2B], [rs2B])
                  for tl in range(2):
                      s = xslot(i, tl)
                      DVE(lambda: nc.vector.scalar_tensor_tensor(out=xs[:], in0=xt[s][:], scalar=rs2[:, tl:tl + 1], in1=gbc[:],
                                                                 op0=ALU.mult, op1=ALU.mult), [xtB[s], rs2B, gbcB], [xsB])
                      pg, pgb = gfull()
                      pg16 = pg.bitcast(BF16)
                      for kc in range(8):
                          PE(lambda: nc.tensor.transpose(pg16[:, kc * 128:(kc + 1) * 128], xs[:, kc * 128:(kc + 1) * 128], identb),
                             [xsB, cstbB], pgb)
                      DVE(lambda: nc.vector.tensor_copy(out=hT3[:, :, tl * 128:(tl + 1) * 128],
                                                        in_=pg16.rearrange("p (k t) -> p k t", k=8)), pgb, [hTB])
                  stage("A_inproj")
                  wslot, wbuf = w_get(l, i, 0)
                  wv = wslot[:].rearrange("p (c k j) -> p c k j", c=4, k=8)
                  for tl in range(2):
                      tt = i * 2 + tl
                      sl = tt % 4
                      pg, pgb = gfull()
                      for ko in range(8):
                          PE(lambda: nc.tensor.matmul(pg.rearrange("p (c j) -> p c j", c=4), lhsT=hT3[:, ko, tl * 128:(tl + 1) * 128],
                                                      rhs=wv[:, :, ko, :], start=(ko == 0), stop=(ko == 7)), [wbuf, hTB], pgb)
                      DVE(lambda: nc.vector.tensor_copy(out=ax[:, sl * 512:(sl + 1) * 512], in_=pg), pgb, [axB[sl]])
                  wslot, wbuf = w_get(l, i, 1)
                  fm_run([(wslot, wbuf, c * 1024, 0, 128, silu2_evac(sz2a[:, c * 256:(c + 1) * 256], sz2aB, 128)) for c in range(4)])
                  gate_chunks(l, i, 2)
                  for tl in range(2):
                      tt = i * 2 + tl
                      sl = tt % 4
                      sp_ = (tt - 1) % 4
                      pg, pgb = gfull()
                      for g in range(4):
                          o = pg[:, g * 128:(g + 1) * 128]
                          bcur = cstf[:, CF_BAND + (g * 3 + (2 if tt == 0 else 0)) * 128: CF_BAND + (g * 3 + (2 if tt == 0 else 0)) * 128 + 128]
                          bprv = cstf[:, CF_BAND + (g * 3 + 1) * 128: CF_BAND + (g * 3 + 1) * 128 + 128]
                          if tt > 0:
                              PE(lambda: nc.tensor.matmul(o, lhsT=ax[:, sp_ * 512 + g * 128: sp_ * 512 + (g + 1) * 128], rhs=bprv,
                                                          start=True, stop=False), [axB[sp_], cstfB], pgb)
                          PE(lambda: nc.tensor.matmul(o, lhsT=ax[:, sl * 512 + g * 128: sl * 512 + (g + 1) * 128], rhs=bcur,
                                                      start=(tt == 0), stop=True), [axB[sl], cstfB], pgb)
                      DVE(lambda: nc.vector.tensor_copy(out=pooledT[:], in_=pg), pgb, [pooledTB])
                      pg2, pgb2 = gfull()
                      for g in range(4):
                          PE(lambda: nc.tensor.matmul(pg2[:, g * 128:(g + 1) * 128], lhsT=pw[:, g * 128:(g + 1) * 128],
                                                      rhs=pooledT[:, g * 128:(g + 1) * 128], start=True, stop=True), [pwB, pooledTB], pgb2)
                      for g in range(4):
                          DVE(lambda: nc.vector.scalar_tensor_tensor(
                              out=yaT[:, g * 256 + tl * 128: g * 256 + (tl + 1) * 128], in0=pg2[:, g * 128:(g + 1) * 128],
                              scalar=pscale[:, g:g + 1], in1=sz2a[:, g * 256 + tl * 128: g * 256 + (tl + 1) * 128],
                              op0=ALU.mult, op1=ALU.mult), pgb2 + [pscB, sz2aB], [yaTB])
                  wslot, wbuf = w_get(l, i, 4)
                  proj_merge(True, 4,
                             lambda k, dc, wslot=wslot: wslot[:, k * 1024 + dc * 128: k * 1024 + (dc + 1) * 128],
                             lambda k: yaT[:, k * 256:(k + 1) * 256], [yaTB], lambda k, wbuf=wbuf: wbuf)
                  stage("B_inproj")
                  wslot, wbuf = w_get(l, i, 5)
                  specs = []
                  for c in range(4):
                      def ev(pa, pb, c=c):
                          for hf in range(2):
                              hh = 2 * c + hf
                              ACT(lambda: nc.scalar.activation(out=QTb[:, hh * 256:(hh + 1) * 256], in_=pa[hf * 64:(hf + 1) * 64, :],
                                                               func=AF.Identity, scale=0.125), pb, [QTbB])
                      specs.append((wslot, wbuf, c * 1024, 0, 128, ev))
                  fm_run(specs)
                  wslot, wbuf = w_get(l, i, 6)
                  s0 = (i * 2) % 4

                  def evk(pa, pb):
                      for g in range(2):
                          ACT(lambda: nc.scalar.copy(out=KTb[:, g * 512 + s0 * 128: g * 512 + (s0 + 2) * 128], in_=pa[g * 64:(g + 1) * 64, :]),
                              pb, [KTbB[s0], KTbB[s0 + 1]])
                  fm_chunk(wslot, wbuf, 0, 0, 128, evk)
                  wv = wslot[:].rearrange("p (c k j) -> p c k j", c=4, k=8)
                  for tl in range(2):
                      sl = (i * 2 + tl) % 4
                      pg, pgb = ghalf()
                      for ko in range(8):
                          PE(lambda: nc.tensor.matmul(pg[:, 0:128], lhsT=hT3[:, ko, tl * 128:(tl + 1) * 128], rhs=wv[:, 1, ko, :],
                                                      start=(ko == 0), stop=(ko == 7)), [wbuf, hTB], pgb)
                      DVE(lambda: nc.vector.tensor_copy(
                          out=Vb[:, sl * 130:(sl + 1) * 130].rearrange("p (g e) -> p g e", g=2)[:, :, 0:64],
                          in_=pg[:, 0:128].rearrange("p (g e) -> p g e", g=2)), pgb, [VbB[sl]])
                  wslot, wbuf = w_get(l, i, 7)
                  fm_run([(wslot, wbuf, (hh // 2) * 1024, (hh % 2) * 64, 64, silu2_evac(sz2b[:, hh * 256:(hh + 1) * 256], sz2bB, 64)) for hh in range(8)])
                  gate_chunks(l, i, 8)
                  stage("B_swa")
                  QTb3 = QTb[:].rearrange("p (c t) -> p c t", c=8)
                  ybT3 = ybT[:].rearrange("p (h t) -> p h t", h=8)
                  sz2b3 = sz2b[:].rearrange("p (h t) -> p h t", h=8)
                  for tl in range(2):
                      tt = i * 2 + tl
                      sl = tt % 4
                      sp_ = (tt - 1) % 4
                      for g in range(2):
                          qv = QTb3[:, g * 4:(g + 1) * 4, tl * 128:(tl + 1) * 128]
                          PE(lambda: nc.tensor.matmul(PS[0][:, :].rearrange("p (c t) -> p c t", c=4), lhsT=KTb[:, g * 512 + sl * 128: g * 512 + (sl + 1) * 128],
                                                      rhs=qv, start=True, stop=False), [KTbB[sl], QTbB], [PSB[0]])
                          PE(lambda: nc.tensor.matmul(PS[0][:, :], lhsT=identb, rhs=cstb[:, CB_OWN4:CB_OWN4 + 512], start=False, stop=True),
                             [cstbB], [PSB[0]])
                          ACT(lambda: nc.scalar.activation(out=PT[0][:], in_=PS[0][:, :], func=AF.Exp), [PSB[0]], [PTB[0]])
                          if tt > 0:
                              PE(lambda: nc.tensor.matmul(PS[1][:, :].rearrange("p (c t) -> p c t", c=4), lhsT=KTb[:, g * 512 + sp_ * 128: g * 512 + (sp_ + 1) * 128],
                                                          rhs=qv, start=True, stop=False), [KTbB[sp_], QTbB], [PSB[1]])
                              PE(lambda: nc.tensor.matmul(PS[1][:, :], lhsT=identb, rhs=cstb[:, CB_PRV4:CB_PRV4 + 512], start=False, stop=True),
                                 [cstbB], [PSB[1]])
                              ACT(lambda: nc.scalar.activation(out=PT[1][:], in_=PS[1][:, :], func=AF.Exp), [PSB[1]], [PTB[1]])
                          po = PO[g][0:65, :]
                          if tt > 0:
                              PE(lambda: nc.tensor.matmul(po, lhsT=Vb[:, sp_ * 130 + g * 65: sp_ * 130 + (g + 1) * 65], rhs=PT[1][:],
                                                          start=True, stop=False), [VbB[sp_], PTB[1]], [POB[g]])
                          PE(lambda: nc.tensor.matmul(po, lhsT=Vb[:, sl * 130 + g * 65: sl * 130 + (g + 1) * 65], rhs=PT[0][:],
                                                      start=(tt == 0), stop=False), [VbB[sl], PTB[0]], [POB[g]])
                          PE(lambda: nc.tensor.matmul(po, lhsT=cstf[0:1, CF_E64:CF_E64 + 65], rhs=esink[0:1, g * 512:(g + 1) * 512],
                                                      start=False, stop=True), [cstfB, esinkB], [POB[g]])
                          normalize(po, [POB[g]], 512, ybT3[:, g * 4:(g + 1) * 4, tl * 128:(tl + 1) * 128], ybTB,
                                    sz2b3[:, g * 4:(g + 1) * 4, tl * 128:(tl + 1) * 128], sz2bB, g)
                  wsl0, wb0 = w_get(l, i, 10)
                  wsl1, wb1 = w_get(l, i, 11)
                  proj_merge(False, 8,
                             lambda k, dc: (wsl0 if k < 4 else wsl1)[0:64, (k % 4) * 1024 + dc * 128:(k % 4) * 1024 + (dc + 1) * 128],
                             lambda k: ybT[:, k * 256:(k + 1) * 256], [ybTB], lambda k: (wb0 if k < 4 else wb1))
                  stage("C_inproj")
                  kvq = list(range(i))
                  kv_next = {"n": 0}

                  def kv_issue():
                      n = kv_next["n"]
                      if n >= len(kvq):
                          return
                      j = kvq[n]
                      slot = kv_state.setdefault("cnt", 0) % NKV
                      kv_state["cnt"] += 1
                      kv_state[("slot", i, j)] = slot
                      T.dma("pool", "kvl%d" % slot, kvK[slot][:], kcs[l * NB + j], reads=[kvsB[l * NB + j]], writes=[kvB[slot]])
                      T.dma("pool", "kvl%d" % slot, kvV[slot][:], vcs[l * NB + j], reads=[kvsB[l * NB + j]], writes=[kvB[slot]])
                      kv_next["n"] += 1
                  for _ in range(NKV - 1):
                      kv_issue()
                  wslot, wbuf = w_get(l, i, 12)
                  specs = []
                  for c in range(4):
                      def ev(pa, pb, c=c):
                          for hf in range(2):
                              hh = 2 * c + hf
                              ACT(lambda: nc.scalar.activation(out=QTc[:, hh * 256:(hh + 1) * 256], in_=pa[hf * 64:(hf + 1) * 64, :],
                                                               func=AF.Identity, scale=0.125), pb, [QTcB])
                      specs.append((wslot, wbuf, c * 1024, 0, 128, ev))
                  fm_run(specs)
                  wslot, wbuf = w_get(l, i, 13)
                  specs = []
                  for c in range(4):
                      def ev(pa, pb, c=c):
                          for hf in range(2):
                              hh = 2 * c + hf
                              ACT(lambda: nc.scalar.activation(out=KTcur[:, hh * 256:(hh + 1) * 256], in_=pa[hf * 64:(hf + 1) * 64, :],
                                                               func=AF.Identity, accum_out=ksum[:, hh:hh + 1]), pb, [KTcurB, ksumB])
                      specs.append((wslot, wbuf, c * 1024, 0, 128, ev))
                  fm_run(specs)
                  wslot, wbuf = w_get(l, i, 14)
                  wv = wslot[:].rearrange("p (c k j) -> p c k j", c=4, k=8)
                  for tl in range(2):
                      pg, pgb = gfull()
                      for ko in range(8):
                          PE(lambda: nc.tensor.matmul(pg.rearrange("p (c j) -> p c j", c=4), lhsT=hT3[:, ko, tl * 128:(tl + 1) * 128],
                                                      rhs=wv[:, :, ko, :], start=(ko == 0), stop=(ko == 7)), [wbuf, hTB], pgb)
                      DVE(lambda: nc.vector.tensor_copy(
                          out=Vcur[:, tl * 520:(tl + 1) * 520].rearrange("p (h e) -> p h e", h=8)[:, :, 0:64],
                          in_=pg.rearrange("p (h e) -> p h e", h=8)), pgb, [VcurB])
                  wslot, wbuf = w_get(l, i, 15)
                  fm_run([(wslot, wbuf, (hh // 2) * 1024, (hh % 2) * 64, 64, silu2_evac(sz2c[:, hh * 256:(hh + 1) * 256], sz2cB, 64)) for hh in range(8)])
                  gate_chunks(l, i, 16)
                  if i + 1 < NB:
                      T.dma("pool", "kvst", kcs[l * NB + i], KTcur[:], reads=[KTcurB], writes=[kvsB[l * NB + i]])
                      T.dma("pool", "kvst", vcs[l * NB + i], Vcur[:], reads=[VcurB], writes=[kvsB[l * NB + i]])
                  QTc3 = QTc[:].rearrange("p (c t) -> p c t", c=8)
                  kmT3 = kmT[:].rearrange("p (c j) -> p c j", c=8)
                  selbT3 = selbT[:].rearrange("p (c t) -> p c t", c=8)
                  stage("C_sel")
                  if i >= 1:
                      DVE(lambda: nc.vector.memset(pmask[:].rearrange("p (h j) -> p h j", h=8)[:, :, i - 1:i], 0.0), [], [pmaskB])
                      for tl in range(2):
                          pg, pgb = ghalf()
                          for h in range(8):
                              PE(lambda: nc.tensor.matmul(pg[:, h * 32:(h + 1) * 32], lhsT=QTc3[:, h, tl * 128:(tl + 1) * 128],
                                                          rhs=kmT3[:, h, :], start=True, stop=True), [QTcB, kmTB], pgb)
                          stage("sel_a")
                          DVE(lambda: nc.vector.tensor_tensor(out=gm[:], in0=pg, in1=pmask[:], op=ALU.add), pgb + [pmaskB], [gmB])
                          stage("sel_b")
                          for h in range(8):
                              DVE(lambda: nc.vector.max(out=top8[:, h * 8:(h + 1) * 8], in_=gm[:, h * 32:(h + 1) * 32]), [gmB], [top8B])
                          stage("sel_c")
                          DVE(lambda: nc.vector.tensor_scalar(out=thr[:].rearrange("p (h o) -> p h o", o=1),
                                                              in0=top8[:].rearrange("p (h e) -> p h e", e=8)[:, :, 2:3],
                                                              scalar1=-1e29, scalar2=None, op0=ALU.max), [top8B], [thrB])
                          stage("sel_d")
                          for h in range(8):
                              DVE(lambda: nc.vector.tensor_scalar(out=gm[:, h * 32:(h + 1) * 32], in0=gm[:, h * 32:(h + 1) * 32],
                                                                  scalar1=thr[:, h:h + 1], scalar2=None, op0=ALU.is_ge), [gmB, thrB], [gmB])
                          DVE(lambda: nc.vector.tensor_scalar(out=selm[:], in0=gm[:], scalar1=1.0, scalar2=None, op0=ALU.subtract), [gmB], [selmB])
                          stage("sel_e")
                          pg2, pgb2 = gfull()
                          pg16 = pg2.bitcast(BF16)
                          for h in range(8):
                              PE(lambda: nc.tensor.transpose(pg16[0:32, h * 128:(h + 1) * 128], selm[:, h * 32:(h + 1) * 32], identb),
                                 [selmB, cstbB], pgb2)
                          stage("sel_f")
                          DVE(lambda: nc.vector.tensor_copy(out=selbT3[:, :, tl * 128:(tl + 1) * 128],
                                                            in_=pg16[0:32, :].rearrange("p (c t) -> p c t", c=8)), pgb2, [selbTB])
                  DVE(lambda: nc.vector.tensor_scalar(out=kmT3[:, :, i:i + 1], in0=ksum[:].rearrange("p (c o) -> p c o", o=1),
                                                      scalar1=1.0 / TB, scalar2=None, op0=ALU.mult), [ksumB], [kmTB])
                  stage("C_attn")
                  gst["wide"] = False
                  its = [(j, h) for j in range(i + 1) for h in range(8)]

                  def src_of(j):
                      if j == i:
                          return KTcur, Vcur, [KTcurB, VcurB]
                      slot = kv_state[("slot", i, j)]
                      return kvK[slot], kvV[slot], [kvB[slot]]

                  def emit_qk(n):
                      j, h = its[n]
                      own = (j == i)
                      Kt, Vt, kvb = src_of(j)
                      Kt3 = Kt[:].rearrange("p (c t) -> p c t", c=8)
                      kps, kpt = n % 2, n % 3
                      for kt in range(2):
                          o = PS[kps][:, kt * 256:(kt + 1) * 256]
                          PE(lambda: nc.tensor.matmul(o, lhsT=Kt3[:, h, kt * 128:(kt + 1) * 128], rhs=QTc3[:, h, :],
                                                      start=True, stop=False), kvb + [QTcB], [PSB[kps]])
                          if own:
                              PE(lambda: nc.tensor.matmul(o, lhsT=identb, rhs=cstb[:, CB_CAUS + kt * 256: CB_CAUS + (kt + 1) * 256],
                                                          start=False, stop=True), [cstbB], [PSB[kps]])
                          else:
                              PE(lambda: nc.tensor.matmul(o, lhsT=cstb[0:32, CB_OH + j * 128: CB_OH + (j + 1) * 128],
                                                          rhs=selbT3[:, h, :], start=False, stop=True), [cstbB, selbTB], [PSB[kps]])
                      ACT(lambda: nc.scalar.activation(out=PT[kpt][:], in_=PS[kps][:, :], func=AF.Exp), [PSB[kps]], [PTB[kpt]])

                  def emit_pv(n):
                      j, h = its[n]
                      own = (j == i)
                      Kt, Vt, kvb = src_of(j)
                      kpt = n % 3
                      for kt in range(2):
                          PE(lambda: nc.tensor.matmul(PO[h // 2][0:65, (h % 2) * 256:(h % 2 + 1) * 256],
                                                      lhsT=Vt[:, kt * 520 + h * 65: kt * 520 + (h + 1) * 65],
                                                      rhs=PT[kpt][:, kt * 256:(kt + 1) * 256],
                                                      start=(j == 0 and kt == 0 and h % 2 == 0), stop=(own and kt == 1 and h % 2 == 1)),
                             kvb + [PTB[kpt]], [POB[h // 2]])
                      if h == 7 and not own:
                          kv_issue()
                  emit_qk(0)
                  for n in range(len(its)):
                      if n + 1 < len(its):
                          emit_qk(n + 1)
                      emit_pv(n)
                  stage("C_norm")
                  ycT3 = ycT[:].rearrange("p (h t) -> p h t", h=8)
                  sz2c3 = sz2c[:].rearrange("p (h t) -> p h t", h=8)
                  for hp in range(4):
                      normalize(PO[hp][0:65, :], [POB[hp]], 512, ycT3[:, hp * 2:(hp + 1) * 2, :], ycTB,
                                sz2c3[:, hp * 2:(hp + 1) * 2, :], sz2cB, hp % 2)
                  gst["wide"] = True
                  wsl0, wb0 = w_get(l, i, 18)
                  wsl1, wb1 = w_get(l, i, 19)
                  proj_merge(False, 8,
                             lambda k, dc: (wsl0 if k < 4 else wsl1)[0:64, (k % 4) * 1024 + dc * 128:(k % 4) * 1024 + (dc + 1) * 128],
                             lambda k: ycT[:, k * 256:(k + 1) * 256], [ycTB], lambda k: (wb0 if k < 4 else wb1))
                  stage("out")
                  DVE(lambda: nc.vector.tensor_copy(out=mT[:], in_=acc[:]), [accB], [mTB])
                  wsl0, wb0 = w_get(l, i, 20)
                  wsl1, wb1 = w_get(l, i, 21)
                  for tl in range(2):
                      s = xslot(i, tl)
                      t = i * 2 + tl
                      for cg in range(2):
                          pg, pgb = gfull()
                          for ko in range(8):
                              wsl, wb = (wsl0, wb0) if ko < 4 else (wsl1, wb1)
                              PE(lambda: nc.tensor.matmul(pg, lhsT=mT3[:, ko, tl * 128:(tl + 1) * 128],
                                                          rhs=wsl[:, (ko % 4) * 1024 + cg * 512:(ko % 4) * 1024 + (cg + 1) * 512],
                                                          start=(ko == 0), stop=(ko == 7)), [mTB, wb], pgb)
                          DVE(lambda: nc.vector.scalar_tensor_tensor(out=xt[s][:, cg * 512:(cg + 1) * 512], in0=pg, scalar=0.5,
                                                                     in1=xt[s][:, cg * 512:(cg + 1) * 512], op0=ALU.mult, op1=ALU.add),
                              pgb + [xtB[s]], [xtB[s]])
                      if l + 1 < NL:
                          T.dma("pool", "xst%d" % s, x1[t * 128:(t + 1) * 128, :], xt[s][:], reads=[xtB[s]], writes=[x1B[t]])
                  if l + 1 == NL:
                      for tl in range(2):
                          s = xslot(i, tl)
                          ACT(lambda: nc.scalar.activation(out=sqj[:], in_=xt[s][:], func=AF.Square, accum_out=ss2[:, tl:tl + 1]),
                              [xtB[s]], [sqjB, ss2B])
                      DVE(lambda: nc.vector.tensor_scalar(out=rs2[:], in0=ss2[:], scalar1=1.0 / D, scalar2=EPS, op0=ALU.mult, op1=ALU.add),
                          [ss2B], [rs2B])
                      ACT(lambda: nc.scalar.activation(out=rs2[:], in_=rs2[:], func=AF.Sqrt), [rs2B], [rs2B])
                      DVE(lambda: nc.vector.reciprocal(out=rs2[:], in_=rs2[:]), [rs2B], [rs2B])
                      for tl in range(2):
                          s = xslot(i, tl)
                          t = i * 2 + tl
                          DVE(lambda: nc.vector.scalar_tensor_tensor(out=acc[:, tl * D:(tl + 1) * D], in0=xt[s][:], scalar=rs2[:, tl:tl + 1], in1=fgbc[:],
                                                                     op0=ALU.mult, op1=ALU.mult), [xtB[s], rs2B, fgbcB], [accB])
                          T.dma("pool", "osd%d" % tl, out[t * 128:(t + 1) * 128, :], acc[:, tl * D:(tl + 1) * D], reads=[accB], writes=[outB])
        except _Stop:
            pass
        T.barrier()
        T.close()
    return nc


def _consts():
    cf = np.zeros((128, CF_N), np.float32)
    cf[:, CF_ID:CF_ID + 128] = np.eye(128, dtype=np.float32)
    tp = np.arange(128)[:, None]
    t = np.arange(128)[None, :]
    for g, w in enumerate((2, 4, 8, 16)):
        cur = ((tp <= t) & (tp > t - w)).astype(np.float32) / w - np.eye(128, dtype=np.float32)
        prv = ((tp - 128 > t - w)).astype(np.float32) / w
        cnt = np.minimum(t + 1, w).astype(np.float32)
        fst = ((tp <= t) & (tp > t - w)).astype(np.float32) / cnt - np.eye(128, dtype=np.float32)
        for k, m in enumerate((cur, prv, fst)):
            c0 = CF_BAND + (g * 3 + k) * 128
            cf[:, c0:c0 + 128] = m
    cf[0, CF_E64 + 64] = 1.0
    cf[:, CF_ONES:CF_ONES + 64] = 1.0
    cf[64, CF_S65:CF_S65 + 64] = 1.0
    cb = np.zeros((128, CB_N), np.float32)
    cb[:, CB_ID:CB_ID + 128] = np.eye(128)
    k = np.arange(128)[:, None]
    q = np.arange(256)[None, :]
    for kt in range(2):
        cb[:, CB_CAUS + kt * 256: CB_CAUS + (kt + 1) * 256] = np.where(kt * 128 + k <= q, 0.0, NEG)
    q1 = np.arange(128)[None, :]
    own = np.where(k <= q1, 0.0, NEG)
    prv = np.where(k > q1, 0.0, NEG)
    for c in range(4):
        cb[:, CB_OWN4 + c * 128: CB_OWN4 + (c + 1) * 128] = own
        cb[:, CB_PRV4 + c * 128: CB_PRV4 + (c + 1) * 128] = prv
    for p in range(32):
        j = p
        cb[p, CB_OH + j * 128: CB_OH + (j + 1) * 128] = -NEG
    return cf, cb.astype(ml_dtypes.bfloat16)


def _inproj_unit(w, cols):
    cols = np.asarray(cols)
    sub = np.zeros((1024, 512), np.float32)
    ok = cols >= 0
    sub[:, ok] = w[:, cols[ok]]
    u = sub.reshape(8, 128, 4, 128).transpose(1, 2, 0, 3)
    return np.ascontiguousarray(u).reshape(128, UE)


def _layer_units(w_in, wpa, wpb, wpc, wout):
    r = np.arange
    A_X, A_Z, B_Q, B_K, B_V, B_Z, C_Q, C_K, C_V, C_Z, GT = 0, 512, 1024, 1536, 1664, 1792, 2304, 2816, 3328, 3840, 4352
    units = []
    units.append(_inproj_unit(w_in, A_X + r(512)))
    units.append(_inproj_unit(w_in, A_Z + r(512)))
    units.append(_inproj_unit(w_in, GT + r(512)))
    units.append(_inproj_unit(w_in, GT + 512 + r(512)))
    units.append(np.ascontiguousarray(wpa.reshape(4, 128, 1024).transpose(1, 0, 2)).reshape(128, UE))
    units.append(_inproj_unit(w_in, B_Q + r(512)))
    units.append(_inproj_unit(w_in, np.concatenate([B_K + r(128), B_V + r(128), -np.ones(256, np.int64)])))
    units.append(_inproj_unit(w_in, B_Z + r(512)))
    units.append(_inproj_unit(w_in, GT + 1024 + r(512)))
    units.append(_inproj_unit(w_in, GT + 1536 + r(512)))
    for wp in (wpb,):
        for half in range(2):
            u = np.zeros((128, UE), np.float32)
            u[0:64] = wp.reshape(8, 64, 1024)[half * 4:(half + 1) * 4].transpose(1, 0, 2).reshape(64, UE)
            units.append(u)
    units.append(_inproj_unit(w_in, C_Q + r(512)))
    units.append(_inproj_unit(w_in, C_K + r(512)))
    units.append(_inproj_unit(w_in, C_V + r(512)))
    units.append(_inproj_unit(w_in, C_Z + r(512)))
    units.append(_inproj_unit(w_in, GT + 2048 + r(512)))
    units.append(_inproj_unit(w_in, GT + 2560 + r(512)))
    for half in range(2):
        u = np.zeros((128, UE), np.float32)
        u[0:64] = wpc.reshape(8, 64, 1024)[half * 4:(half + 1) * 4].transpose(1, 0, 2).reshape(64, UE)
        units.append(u)
    wo = wout.reshape(8, 128, 1024).transpose(1, 0, 2)
    units.append(np.ascontiguousarray(wo[:, 0:4]).reshape(128, UE))
    units.append(np.ascontiguousarray(wo[:, 4:8]).reshape(128, UE))
    assert len(units) == NU
    return np.stack(units)


def prep_shared(norm_g, w_in, pool_w, pool_scale, sink_logits, w_proj_a, w_proj_b, w_proj_c, w_out, final_norm_g):
    NL = w_in.shape[0]
    f = lambda a: np.asarray(a, np.float32)
    norm_g, w_in, pool_w, pool_scale, sink_logits = f(norm_g), f(w_in), f(pool_w), f(pool_scale), f(sink_logits)
    w_proj_a, w_proj_b, w_proj_c, w_out, final_norm_g = f(w_proj_a), f(w_proj_b), f(w_proj_c), f(w_out), f(final_norm_g)
    wall = np.concatenate([_layer_units(w_in[l], w_proj_a[l], w_proj_b[l], w_proj_c[l], w_out[l]) for l in range(NL)])
    gbc = np.stack([np.broadcast_to(norm_g[l], (128, D)) for l in range(NL)] + [np.broadcast_to(final_norm_g, (128, D))])
    poolw = np.stack([np.ascontiguousarray(pool_w[l].transpose(1, 0, 2)).reshape(128, 512) for l in range(NL)])
    pscale = np.stack([np.ascontiguousarray(pool_scale[l].reshape(4, 128).T) for l in range(NL)])
    sink = np.stack([np.repeat(sink_logits[l], 128).reshape(1, 1024) for l in range(NL)])
    cf, cb = _consts()
    return {"wall": np.ascontiguousarray(wall, np.float32), "gbc": np.ascontiguousarray(gbc, np.float32),
            "poolw": np.ascontiguousarray(poolw, np.float32), "pscale": np.ascontiguousarray(pscale, np.float32),
            "sink": np.ascontiguousarray(sink, np.float32), "cstf": cf, "cstb": cb}


def run(x, shared, n_cores):
    B, S, _ = x.shape
    NL = shared["wall"].shape[0] // NU
    nc = build_nc(S, NL)
    in_maps = []
    for c in range(n_cores):
        m = dict(shared)
        m["x"] = np.ascontiguousarray(x[c % B], np.float32)
        in_maps.append(m)
    res = run_bass_kernel_spmd(nc, in_maps, core_ids=list(range(n_cores)))
    return np.stack([np.asarray(res.results[c]["out"]) for c in range(B)]).astype(np.float32)


def kernel(x, norm_g, w_in, pool_w, pool_scale, sink_logits, w_proj_a, w_proj_b, w_proj_c, w_out, final_norm_g):
    x = np.asarray(x, np.float32)
    shared = prep_shared(norm_g, w_in, pool_w, pool_scale, sink_logits, w_proj_a, w_proj_b, w_proj_c, w_out, final_norm_g)
    return run(x, shared, x.shape[0])
```

```python
import numpy as np
import ml_dtypes
from contextlib import ExitStack
import concourse.bass as bass
import concourse.mybir as mybir
from concourse.bass_utils import run_bass_kernel_spmd

F32 = mybir.dt.float32
BF16 = mybir.dt.bfloat16
AF = mybir.ActivationFunctionType
ALU = mybir.AluOpType

D = 1024
TB = 256
NU = 22
UE = 4096
NEG = -30000.0
NWS = 5
NKV = 3
EPS = 1e-6

CF_ID = 0
CF_BAND = 128
CF_E64 = CF_BAND + 12 * 128
CF_ONES = CF_E64 + 65
CF_S65 = CF_ONES + 64
CF_N = CF_S65 + 64
CB_ID = 0
CB_CAUS = 128
CB_OWN4 = CB_CAUS + 512
CB_PRV4 = CB_OWN4 + 512
CB_OH = CB_PRV4 + 512
CB_N = CB_OH + 32 * 128


class Buf:
    __slots__ = ("name", "w", "r")

    def __init__(self, name):
        self.name = name
        self.w = None
        self.r = []


class Trk:
    def __init__(self, nc):
        self.nc = nc
        self.engs = {"pe": nc.tensor, "act": nc.scalar, "dve": nc.vector, "pool": nc.gpsimd, "sp": nc.sync}
        self.sems = {}
        self.cnt = {}
        self.waited = {}
        self._stack = []
        for name in self.engs:
            self._newsem("E_" + name)

    def _newsem(self, key):
        cm = self.nc.semaphore(key)
        h = cm.__enter__()
        self._stack.append(cm)
        self.sems[key] = h
        self.cnt[key] = 0
        return h

    def close(self):
        for cm in reversed(self._stack):
            cm.__exit__(None, None, None)

    def _wait(self, engname, tok):
        if tok is None:
            return
        key, val = tok
        if self.waited.get((engname, key), 0) >= val:
            return
        self.engs[engname].wait_ge(self.sems[key], val)
        self.waited[(engname, key)] = val

    def _deps(self, engname, reads, writes):
        best = {}

        def add(t):
            if t is None:
                return
            if engname == "pe" and t[0] == "E_pe":
                return
            if best.get(t[0], 0) < t[1]:
                best[t[0]] = t[1]
        for b in reads:
            add(b.w)
        for b in writes:
            add(b.w)
            for t in b.r:
                add(t)
        for k, v in best.items():
            self._wait(engname, (k, v))

    def _commit(self, tok, reads, writes):
        for b in writes:
            b.w = tok
            b.r = []
        for b in reads:
            b.r.append(tok)
            if len(b.r) > 24:
                m = {}
                for t in b.r:
                    if m.get(t[0], 0) < t[1]:
                        m[t[0]] = t[1]
                b.r = list(m.items())

    def op(self, engname, fn, reads=(), writes=()):
        self._deps(engname, reads, writes)
        ins = fn()
        key = "E_" + engname
        ins.then_inc(self.sems[key], 1)
        self.cnt[key] += 1
        tok = (key, self.cnt[key])
        self._commit(tok, reads, writes)
        return tok

    def dma(self, qname, semkey, out, in_, reads=(), writes=()):
        if semkey not in self.sems:
            self._newsem(semkey)
        self._deps(qname, reads, writes)
        ins = self.engs[qname].dma_start(out=out, in_=in_)
        ins.then_inc(self.sems[semkey], 16)
        self.cnt[semkey] += 16
        tok = (semkey, self.cnt[semkey])
        self._commit(tok, reads, writes)
        return tok

    def barrier(self):
        toks = [(k, v) for k, v in self.cnt.items() if v > 0]
        for e in self.engs:
            for t in toks:
                self._wait(e, t)


class _Stop(Exception):
    pass


def build_nc(S, NL, stop=None):
    import os
    stop = stop or os.environ.get("MK_STOP")

    _hits = {}

    def stage(name):
        _hits[name] = _hits.get(name, 0) + 1
        if stop and stop.split("#")[0] == name and _hits[name] == int((stop + "#1").split("#")[1]):
            raise _Stop()
    NB = S // TB
    NT = S // 128
    nc = bass.Bass("TRN2", target_bir_lowering=False)
    x_in = nc.dram_tensor("x", [S, D], F32, kind="ExternalInput").ap()
    wall = nc.dram_tensor("wall", [NL * NU, 128, UE], F32, kind="ExternalInput").ap()
    gbc_in = nc.dram_tensor("gbc", [NL + 1, 128, D], F32, kind="ExternalInput").ap()
    poolw_in = nc.dram_tensor("poolw", [NL, 128, 512], F32, kind="ExternalInput").ap()
    pscale_in = nc.dram_tensor("pscale", [NL, 128, 4], F32, kind="ExternalInput").ap()
    sink_in = nc.dram_tensor("sink", [NL, 1, 1024], F32, kind="ExternalInput").ap()
    cstf_in = nc.dram_tensor("cstf", [128, CF_N], F32, kind="ExternalInput").ap()
    cstb_in = nc.dram_tensor("cstb", [128, CB_N], BF16, kind="ExternalInput").ap()
    out = nc.dram_tensor("out", [S, D], F32, kind="ExternalOutput").ap()
    wsc = nc.dram_tensor("wsc", [NL * NU, 128, UE], BF16, kind="Internal").ap()
    x1 = nc.dram_tensor("x1s", [S, D], F32, kind="Internal").ap()
    kcs = nc.dram_tensor("kcs", [NL * NB, 64, 2048], BF16, kind="Internal").ap()
    vcs = nc.dram_tensor("vcs", [NL * NB, 128, 1040], BF16, kind="Internal").ap()

    T = Trk(nc)
    PE = lambda fn, r=(), w=(): T.op("pe", fn, r, w)
    ACT = lambda fn, r=(), w=(): T.op("act", fn, r, w)
    DVE = lambda fn, r=(), w=(): T.op("dve", fn, r, w)
    POOL = lambda fn, r=(), w=(): T.op("pool", fn, r, w)

    wscB = [Buf("wsc%d" % u) for u in range(NL * NU)]
    with ExitStack() as es:
        stf = [es.enter_context(nc.sbuf_tensor("stf%d" % k, [128, UE], F32)) for k in range(2)]
        stb = [es.enter_context(nc.sbuf_tensor("stb%d" % k, [128, UE], BF16)) for k in range(2)]
        stfB = [Buf("stf%d" % k) for k in range(2)]
        stbB = [[Buf("stb%da" % k), Buf("stb%db" % k)] for k in range(2)]
        for u in range(NL * NU):
            k = u % 2
            T.dma("sp", "pl%d" % k, stf[k][:], wall[u], writes=[stfB[k]])
            h = UE // 2
            DVE(lambda: nc.vector.tensor_copy(out=stb[k][:, 0:h], in_=stf[k][:, 0:h]), [stfB[k]], [stbB[k][0]])
            ACT(lambda: nc.scalar.copy(out=stb[k][:, h:UE], in_=stf[k][:, h:UE]), [stfB[k]], [stbB[k][1]])
            T.dma("pool", "ps%d" % k, wsc[u], stb[k][:], reads=stbB[k], writes=[wscB[u]])
        T.barrier()
    if stop == "prologue":
        T.close()
        return nc

    with ExitStack() as es:
        def sb(name, shape, dt=F32):
            return es.enter_context(nc.sbuf_tensor("s_" + name, shape, dt))

        def psum(name):
            return es.enter_context(nc.psum_tensor("p_" + name, [128, 512], F32))

        cstf = sb("cstf", [128, CF_N]); cstfB = Buf("cstf")
        cstb = sb("cstb", [128, CB_N], BF16); cstbB = Buf("cstb")
        identf = cstf[:, CF_ID:CF_ID + 128]
        identb = cstb[:, CB_ID:CB_ID + 128]
        gbc = sb("gbc", [128, D]); gbcB = Buf("gbc")
        fgbc = sb("fgbc", [128, D]); fgbcB = Buf("fgbc")
        pwf = sb("pwf", [128, 512]); pwfB = Buf("pwf")
        pw = sb("pw", [128, 512], BF16); pwB = Buf("pw")
        pscale = sb("pscale", [128, 4]); pscB = Buf("pscale")
        esink = sb("esink", [1, 1024]); esinkB = Buf("esink")
        xt = [sb("xt%d" % k, [128, D]) for k in range(4)]; xtB = [Buf("xt%d" % k) for k in range(4)]
        xs = sb("xs", [128, D], BF16); xsB = Buf("xs")
        sqj = sb("sqj", [128, D], BF16); sqjB = Buf("sqj")
        ss2 = sb("ss2", [128, 2]); ss2B = Buf("ss2")
        rs2 = sb("rs2", [128, 2]); rs2B = Buf("rs2")
        hT = sb("hT", [128, 8 * TB], BF16); hTB = Buf("hT")
        hT3 = hT[:].rearrange("p (k t) -> p k t", k=8)
        ax = sb("ax", [128, 4 * 512]); axB = [Buf("ax%d" % k) for k in range(4)]
        thz = [sb("thz%d" % k, [128, 256]) for k in range(2)]; thzB = [Buf("thz%d" % k) for k in range(2)]
        sz2a = sb("sz2a", [128, 4 * 256], BF16); sz2aB = Buf("sz2a")
        sz2b = sb("sz2b", [64, 8 * 256], BF16); sz2bB = Buf("sz2b")
        sz2c = sb("sz2c", [64, 8 * 256], BF16); sz2cB = Buf("sz2c")
        thg = sb("thg", [128, 8 * 256], BF16); thgB = Buf("thg")
        pooledT = sb("pooledT", [128, 512], BF16); pooledTB = Buf("pooledT")
        yaT = sb("yaT", [128, 4 * 256], BF16); yaTB = Buf("yaT")
        QTb = sb("QTb", [64, 8 * 256], BF16); QTbB = Buf("QTb")
        KTb = sb("KTb", [64, 2 * 4 * 128], BF16); KTbB = [Buf("KTb%d" % k) for k in range(4)]
        Vb = sb("Vb", [128, 4 * 130], BF16); VbB = [Buf("Vb%d" % k) for k in range(4)]
        ybT = sb("ybT", [64, 8 * 256], BF16); ybTB = Buf("ybT")
        QTc = sb("QTc", [64, 8 * 256], BF16); QTcB = Buf("QTc")
        KTcur = sb("KTcur", [64, 2048], BF16); KTcurB = Buf("KTcur")
        Vcur = sb("Vcur", [128, 1040], BF16); VcurB = Buf("Vcur")
        kvK = [sb("kvK%d" % k, [64, 2048], BF16) for k in range(NKV)]
        kvV = [sb("kvV%d" % k, [128, 1040], BF16) for k in range(NKV)]
        kvB = [Buf("kv%d" % k) for k in range(NKV)]
        kmT = sb("kmT", [64, 8 * 32], BF16); kmTB = Buf("kmT")
        ksum = sb("ksum", [64, 8]); ksumB = Buf("ksum")
        pmask = sb("pmask", [128, 256]); pmaskB = Buf("pmask")
        gm = sb("gm", [128, 256]); gmB = Buf("gm")
        top8 = sb("top8", [128, 64]); top8B = Buf("top8")
        thr = sb("thr", [128, 8]); thrB = Buf("thr")
        selm = sb("selm", [128, 256], BF16); selmB = Buf("selm")
        selbT = sb("selbT", [32, 8 * 256], BF16); selbTB = Buf("selbT")
        PT = [sb("PT%d" % k, [128, 512], BF16) for k in range(5)]; PTB = [Buf("PT%d" % k) for k in range(5)]
        OTs = [sb("OTs%d" % k, [65, 512]) for k in range(2)]; OTsB = [Buf("OTs%d" % k) for k in range(2)]
        rec = sb("rec", [64, 512]); recB = Buf("rec")
        tmpn = sb("tmpn", [64, 512]); tmpnB = Buf("tmpn")
        ycT = sb("ycT", [64, 8 * 256], BF16); ycTB = Buf("ycT")
        acc = sb("acc", [128, 8 * 256]); accB = Buf("acc")
        tmpg = [sb("tmpg%d" % k, [128, 256]) for k in range(2)]; tmpgB = [Buf("tmpg%d" % k) for k in range(2)]
        mT = sb("mT", [128, 8 * 256], BF16); mTB = Buf("mT")
        mT3 = mT[:].rearrange("p (k t) -> p k t", k=8)
        ws = [sb("ws%d" % k, [128, UE], BF16) for k in range(NWS)]; wsB = [Buf("ws%d" % k) for k in range(NWS)]
        PS = [psum("PS%d" % k) for k in range(2)]; PSB = [Buf("PS%d" % k) for k in range(2)]
        PO = [psum("PO%d" % k) for k in range(4)]; POB = [Buf("PO%d" % k) for k in range(4)]
        G = [psum("G%d" % k) for k in range(2)]; GB = [Buf("G%d" % k) for k in range(2)]
        x1B = [Buf("x1_%d" % k) for k in range(NT)]
        kvsB = [Buf("kvs%d" % k) for k in range(NL * NB)]
        outB = Buf("out")

        gst = {"n": 0, "wide": True}
        GBK = [(G[0], GB[0]), (G[1], GB[1]), (PO[2], POB[2]), (PO[3], POB[3])]

        def galloc():
            nb = 4 if gst["wide"] else 2
            k = gst["n"] % nb
            gst["n"] += 1
            t, b = GBK[k]
            return t[:, 0:512], [b]

        def ghalf():
            pa, pb = galloc()
            return pa[:, 0:256], pb

        gfull = galloc

        wst = {"next": 0}
        total_units = NL * NB * NU

        def w_issue(gidx):
            l = gidx // (NB * NU)
            u = gidx % NU
            slot = gidx % NWS
            src = wsc[l * NU + u]
            if u in (10, 11, 18, 19):
                T.dma("sp", "wl%d" % slot, ws[slot][0:64, :], src[0:64, :], reads=[wscB[l * NU + u]], writes=[wsB[slot]])
            else:
                T.dma("sp", "wl%d" % slot, ws[slot][:], src, reads=[wscB[l * NU + u]], writes=[wsB[slot]])

        def w_get(l, i, u):
            gidx = (l * NB + i) * NU + u
            while wst["next"] <= min(gidx + NWS - 2, total_units - 1):
                w_issue(wst["next"])
                wst["next"] += 1
            slot = gidx % NWS
            return ws[slot], wsB[slot]

        T.dma("pool", "cst0", cstf[:], cstf_in, writes=[cstfB])
        T.dma("pool", "cst1", cstb[:], cstb_in, writes=[cstbB])
        T.dma("pool", "cst2", fgbc[:], gbc_in[NL], writes=[fgbcB])
        POOL(lambda: nc.gpsimd.memset(kmT[:], 0.0), [], [kmTB])
        POOL(lambda: nc.gpsimd.memset(Vcur[:], 1.0), [], [VcurB])
        POOL(lambda: nc.gpsimd.memset(Vb[:], 1.0), [], VbB)

        def xslot(i, tl):
            return (i % 2) * 2 + tl

        def load_x(l, i):
            for tl in range(2):
                t = i * 2 + tl
                s = xslot(i, tl)
                if l == 0:
                    T.dma("pool", "xl%d" % s, xt[s][:], x_in[t * 128:(t + 1) * 128, :], writes=[xtB[s]])
                else:
                    T.dma("pool", "xl%d" % s, xt[s][:], x1[t * 128:(t + 1) * 128, :], reads=[x1B[t]], writes=[xtB[s]])

        def layer_params(l):
            T.dma("pool", "lp0", gbc[:], gbc_in[l], writes=[gbcB])
            T.dma("pool", "lp1", pwf[:], poolw_in[l], writes=[pwfB])
            T.dma("pool", "lp2", pscale[:], pscale_in[l], writes=[pscB])
            T.dma("pool", "lp3", esink[:], sink_in[l], writes=[esinkB])
            DVE(lambda: nc.vector.tensor_copy(out=pw[:], in_=pwf[:]), [pwfB], [pwB])
            DVE(lambda: nc.vector.tensor_scalar(out=pscale[:], in0=pscale[:], scalar1=0.5, scalar2=None, op0=ALU.mult), [pscB], [pscB])
            ACT(lambda: nc.scalar.activation(out=esink[:], in_=esink[:], func=AF.Exp), [esinkB], [esinkB])
            POOL(lambda: nc.gpsimd.memset(pmask[:], -1e30), [], [pmaskB])

        def fm_run(specs):
            for p0 in range(0, len(specs), 2):
                pair = specs[p0:p0 + 2]
                bank, pb = galloc()
                for idx, (wslot, wbuf, cbase, j0, jn, evac) in enumerate(pair):
                    pa = bank[:, idx * 256:(idx + 1) * 256]
                    for ko in range(8):
                        c0 = cbase + ko * 128 + j0
                        PE(lambda: nc.tensor.matmul(pa[0:jn, :], lhsT=wslot[:, c0:c0 + jn], rhs=hT3[:, ko, :],
                                                    start=(ko == 0), stop=(ko == 7)), [wbuf, hTB], pb)
                for idx, (wslot, wbuf, cbase, j0, jn, evac) in enumerate(pair):
                    evac(bank[:, idx * 256:(idx + 1) * 256], pb)

        def fm_chunk(wslot, wbuf, cbase, j0, jn, evac):
            fm_run([(wslot, wbuf, cbase, j0, jn, evac)])

        def silu2_evac(dst_ap, dstB, np_):
            st = {"k": 0}

            def ev(pa, pb):
                k = st["k"] % 2
                st["k"] += 1
                ACT(lambda: nc.scalar.activation(out=thz[k][0:np_, :], in_=pa[0:np_, :], func=AF.Tanh, scale=0.5), pb, [thzB[k]])
                DVE(lambda: nc.vector.scalar_tensor_tensor(out=dst_ap, in0=thz[k][0:np_, :], scalar=1.0, in1=pa[0:np_, :],
                                                           op0=ALU.add, op1=ALU.mult), [thzB[k]] + pb, [dstB])
            return ev

        def gate_chunks(l, i, u0):
            for uu in range(2):
                wslot, wbuf = w_get(l, i, u0 + uu)
                specs = []
                for cc in range(4):
                    c = uu * 4 + cc

                    def ev(pa, pb, c=c):
                        ACT(lambda: nc.scalar.activation(out=thg[:, c * 256:(c + 1) * 256], in_=pa, func=AF.Tanh, scale=0.5), pb, [thgB])
                    specs.append((wslot, wbuf, cc * 1024, 0, 128, ev))
                fm_run(specs)

        def proj_merge(first, nk, lhs_of, rhs_of, rbufs, wbufs_of):
            for dc in range(8):
                pa, pb = ghalf()
                for k in range(nk):
                    PE(lambda: nc.tensor.matmul(pa, lhsT=lhs_of(k, dc), rhs=rhs_of(k), start=(k == 0), stop=(k == nk - 1)),
                       rbufs + [wbufs_of(k)], pb)
                a_sl = acc[:, dc * 256:(dc + 1) * 256]
                g_sl = thg[:, dc * 256:(dc + 1) * 256]
                if first:
                    DVE(lambda: nc.vector.scalar_tensor_tensor(out=a_sl, in0=g_sl, scalar=1.0, in1=pa, op0=ALU.add, op1=ALU.mult),
                        [thgB] + pb, [accB])
                else:
                    k2 = dc % 2
                    DVE(lambda: nc.vector.scalar_tensor_tensor(out=tmpg[k2][:], in0=g_sl, scalar=1.0, in1=pa, op0=ALU.add, op1=ALU.mult),
                        [thgB] + pb, [tmpgB[k2]])
                    DVE(lambda: nc.vector.tensor_tensor(out=a_sl, in0=a_sl, in1=tmpg[k2][:], op=ALU.add), [accB, tmpgB[k2]], [accB])

        def normalize(po_ap, pobufs, nq, dst3, dstB, sz3, szB, k2):
            n = nq
            DVE(lambda: nc.vector.tensor_copy(out=OTs[k2][:, 0:n], in_=po_ap), pobufs, [OTsB[k2]])
            pg, pgb = gfull()
            PE(lambda: nc.tensor.matmul(pg[0:64, 0:n], lhsT=cstf[0:65, CF_S65:CF_S65 + 64], rhs=OTs[k2][0:65, 0:n],
                                        start=True, stop=True), [OTsB[k2], cstfB], pgb)
            DVE(lambda: nc.vector.reciprocal(out=rec[:, 0:n], in_=pg[0:64, 0:n]), pgb, [recB])
            DVE(lambda: nc.vector.scalar_tensor_tensor(out=tmpn[:, 0:n], in0=OTs[k2][0:64, 0:n], scalar=0.5, in1=rec[:, 0:n],
                                                       op0=ALU.mult, op1=ALU.mult), [OTsB[k2], recB], [tmpnB])
            a = dst3.shape[1]
            DVE(lambda: nc.vector.tensor_tensor(out=dst3, in0=tmpn[:, 0:n].rearrange("p (a b) -> p a b", a=a), in1=sz3, op=ALU.mult),
                [tmpnB, szB], [dstB])

        try:
          for l in range(NL):
              stage("init")
              layer_params(l)
              load_x(l, 0)
              kv_state = {}
              for i in range(NB):
                  if i + 1 < NB:
                      load_x(l, i + 1)
                  stage("norm")
                  for tl in range(2):
                      s = xslot(i, tl)
                      ACT(lambda: nc.scalar.activation(out=sqj[:], in_=xt[s][:], func=AF.Square, accum_out=ss2[:, tl:tl + 1]),
                          [xtB[s]], [sqjB, ss2B])
                  DVE(lambda: nc.vector.tensor_scalar(out=rs2[:], in0=ss2[:], scalar1=1.0 / D, scalar2=EPS, op0=ALU.mult, op1=ALU.add),
                      [ss2B], [rs2B])
                  ACT(lambda: nc.scalar.activation(out=rs2[:], in_=rs2[:], func=AF.Sqrt), [rs2B], [rs2B])
                  DVE(lambda: nc.vector.reciprocal(out=rs2[:], in_=rs2[:]), [rs2B], [rs2B])
                  for tl in range(2):
                      s = xslot(i, tl)
                      DVE(lambda: nc.vector.scalar_tensor_tensor(out=xs[:], in0=xt[s][:], scalar=rs2[:, tl:tl + 1], in1=gbc[:],
                                                                 op0=ALU.mult, op1=ALU.mult), [xtB[s], rs2B, gbcB], [xsB])
                      pg, pgb = gfull()
                      pg16 = pg.bitcast(BF16)
                      for kc in range(8):
                          PE(lambda: nc.tensor.transpose(pg16[:, kc * 128:(kc + 1) * 128], xs[:, kc * 128:(kc + 1) * 128], identb),
                             [xsB, cstbB], pgb)
                      DVE(lambda: nc.vector.tensor_copy(out=hT3[:, :, tl * 128:(tl + 1) * 128],
                                                        in_=pg16.rearrange("p (k t) -> p k t", k=8)), pgb, [hTB])
                  stage("A_inproj")
                  wslot, wbuf = w_get(l, i, 0)
                  wv = wslot[:].rearrange("p (c k j) -> p c k j", c=4, k=8)
                  for tl in range(2):
                      tt = i * 2 + tl
                      sl = tt % 4
                      pg, pgb = gfull()
                      for ko in range(8):
                          PE(lambda: nc.tensor.matmul(pg.rearrange("p (c j) -> p c j", c=4), lhsT=hT3[:, ko, tl * 128:(tl + 1) * 128],
                                                      rhs=wv[:, :, ko, :], start=(ko == 0), stop=(ko == 7)), [wbuf, hTB], pgb)
                      DVE(lambda: nc.vector.tensor_copy(out=ax[:, sl * 512:(sl + 1) * 512], in_=pg), pgb, [axB[sl]])
                  wslot, wbuf = w_get(l, i, 1)
                  fm_run([(wslot, wbuf, c * 1024, 0, 128, silu2_evac(sz2a[:, c * 256:(c + 1) * 256], sz2aB, 128)) for c in range(4)])
                  gate_chunks(l, i, 2)
                  for tl in range(2):
                      tt = i * 2 + tl
                      sl = tt % 4
                      sp_ = (tt - 1) % 4
                      pg, pgb = gfull()
                      for g in range(4):
                          o = pg[:, g * 128:(g + 1) * 128]
                          bcur = cstf[:, CF_BAND + (g * 3 + (2 if tt == 0 else 0)) * 128: CF_BAND + (g * 3 + (2 if tt == 0 else 0)) * 128 + 128]
                          bprv = cstf[:, CF_BAND + (g * 3 + 1) * 128: CF_BAND + (g * 3 + 1) * 128 + 128]
                          if tt > 0:
                              PE(lambda: nc.tensor.matmul(o, lhsT=ax[:, sp_ * 512 + g * 128: sp_ * 512 + (g + 1) * 128], rhs=bprv,
                                                          start=True, stop=False), [axB[sp_], cstfB], pgb)
                          PE(lambda: nc.tensor.matmul(o, lhsT=ax[:, sl * 512 + g * 128: sl * 512 + (g + 1) * 128], rhs=bcur,
                                                      start=(tt == 0), stop=True), [axB[sl], cstfB], pgb)
                      DVE(lambda: nc.vector.tensor_copy(out=pooledT[:], in_=pg), pgb, [pooledTB])
                      pg2, pgb2 = gfull()
                      for g in range(4):
                          PE(lambda: nc.tensor.matmul(pg2[:, g * 128:(g + 1) * 128], lhsT=pw[:, g * 128:(g + 1) * 128],
                                                      rhs=pooledT[:, g * 128:(g + 1) * 128], start=True, stop=True), [pwB, pooledTB], pgb2)
                      for g in range(4):
                          DVE(lambda: nc.vector.scalar_tensor_tensor(
                              out=yaT[:, g * 256 + tl * 128: g * 256 + (tl + 1) * 128], in0=pg2[:, g * 128:(g + 1) * 128],
                              scalar=pscale[:, g:g + 1], in1=sz2a[:, g * 256 + tl * 128: g * 256 + (tl + 1) * 128],
                              op0=ALU.mult, op1=ALU.mult), pgb2 + [pscB, sz2aB], [yaTB])
                  wslot, wbuf = w_get(l, i, 4)
                  proj_merge(True, 4,
                             lambda k, dc, wslot=wslot: wslot[:, k * 1024 + dc * 128: k * 1024 + (dc + 1) * 128],
                             lambda k: yaT[:, k * 256:(k + 1) * 256], [yaTB], lambda k, wbuf=wbuf: wbuf)
                  stage("B_inproj")
                  wslot, wbuf = w_get(l, i, 5)
                  specs = []
                  for c in range(4):
                      def ev(pa, pb, c=c):
                          for hf in range(2):
                              hh = 2 * c + hf
                              ACT(lambda: nc.scalar.activation(out=QTb[:, hh * 256:(hh + 1) * 256], in_=pa[hf * 64:(hf + 1) * 64, :],
                                                               func=AF.Identity, scale=0.125), pb, [QTbB])
                      specs.append((wslot, wbuf, c * 1024, 0, 128, ev))
                  fm_run(specs)
                  wslot, wbuf = w_get(l, i, 6)
                  s0 = (i * 2) % 4

                  def evk(pa, pb):
                      for g in range(2):
                          ACT(lambda: nc.scalar.copy(out=KTb[:, g * 512 + s0 * 128: g * 512 + (s0 + 2) * 128], in_=pa[g * 64:(g + 1) * 64, :]),
                              pb, [KTbB[s0], KTbB[s0 + 1]])
                  fm_chunk(wslot, wbuf, 0, 0, 128, evk)
                  wv = wslot[:].rearrange("p (c k j) -> p c k j", c=4, k=8)
                  for tl in range(2):
                      sl = (i * 2 + tl) % 4
                      pg, pgb = ghalf()
                      for ko in range(8):
                          PE(lambda: nc.tensor.matmul(pg[:, 0:128], lhsT=hT3[:, ko, tl * 128:(tl + 1) * 128], rhs=wv[:, 1, ko, :],
                                                      start=(ko == 0), stop=(ko == 7)), [wbuf, hTB], pgb)
                      DVE(lambda: nc.vector.tensor_copy(
                          out=Vb[:, sl * 130:(sl + 1) * 130].rearrange("p (g e) -> p g e", g=2)[:, :, 0:64],
                          in_=pg[:, 0:128].rearrange("p (g e) -> p g e", g=2)), pgb, [VbB[sl]])
                  wslot, wbuf = w_get(l, i, 7)
                  fm_run([(wslot, wbuf, (hh // 2) * 1024, (hh % 2) * 64, 64, silu2_evac(sz2b[:, hh * 256:(hh + 1) * 256], sz2bB, 64)) for hh in range(8)])
                  gate_chunks(l, i, 8)
                  stage("B_swa")
                  QTb3 = QTb[:].rearrange("p (c t) -> p c t", c=8)
                  ybT3 = ybT[:].rearrange("p (h t) -> p h t", h=8)
                  sz2b3 = sz2b[:].rearrange("p (h t) -> p h t", h=8)
                  for tl in range(2):
                      tt = i * 2 + tl
                      sl = tt % 4
                      sp_ = (tt - 1) % 4
                      for g in range(2):
                          qv = QTb3[:, g * 4:(g + 1) * 4, tl * 128:(tl + 1) * 128]
                          PE(lambda: nc.tensor.matmul(PS[0][:, :].rearrange("p (c t) -> p c t", c=4), lhsT=KTb[:, g * 512 + sl * 128: g * 512 + (sl + 1) * 128],
                                                      rhs=qv, start=True, stop=False), [KTbB[sl], QTbB], [PSB[0]])
                          PE(lambda: nc.tensor.matmul(PS[0][:, :], lhsT=identb, rhs=cstb[:, CB_OWN4:CB_OWN4 + 512], start=False, stop=True),
                             [cstbB], [PSB[0]])
                          ACT(lambda: nc.scalar.activation(out=PT[0][:], in_=PS[0][:, :], func=AF.Exp), [PSB[0]], [PTB[0]])
                          if tt > 0:
                              PE(lambda: nc.tensor.matmul(PS[1][:, :].rearrange("p (c t) -> p c t", c=4), lhsT=KTb[:, g * 512 + sp_ * 128: g * 512 + (sp_ + 1) * 128],
                                                          rhs=qv, start=True, stop=False), [KTbB[sp_], QTbB], [PSB[1]])
                              PE(lambda: nc.tensor.matmul(PS[1][:, :], lhsT=identb, rhs=cstb[:, CB_PRV4:CB_PRV4 + 512], start=False, stop=True),
                                 [cstbB], [PSB[1]])
                              ACT(lambda: nc.scalar.activation(out=PT[1][:], in_=PS[1][:, :], func=AF.Exp), [PSB[1]], [PTB[1]])
                          po = PO[g][0:65, :]
                          if tt > 0:
                              PE(lambda: nc.tensor.matmul(po, lhsT=Vb[:, sp_ * 130 + g * 65: sp_ * 130 + (g + 1) * 65], rhs=PT[1][:],
                                                          start=True, stop=False), [VbB[sp_], PTB[1]], [POB[g]])
                          PE(lambda: nc.tensor.matmul(po, lhsT=Vb[:, sl * 130 + g * 65: sl * 130 + (g + 1) * 65], rhs=PT[0][:],
                                                      start=(tt == 0), stop=False), [VbB[sl], PTB[0]], [POB[g]])
                          PE(lambda: nc.tensor.matmul(po, lhsT=cstf[0:1, CF_E64:CF_E64 + 65], rhs=esink[0:1, g * 512:(g + 1) * 512],
                                                      start=False, stop=True), [cstfB, esinkB], [POB[g]])
                          normalize(po, [POB[g]], 512, ybT3[:, g * 4:(g + 1) * 4, tl * 128:(tl + 1) * 128], ybTB,
                                    sz2b3[:, g * 4:(g + 1) * 4, tl * 128:(tl + 1) * 128], sz2bB, g)
                  wsl0, wb0 = w_get(l, i, 10)
                  wsl1, wb1 = w_get(l, i, 11)
                  proj_merge(False, 8,
                             lambda k, dc: (wsl0 if k < 4 else wsl1)[0:64, (k % 4) * 1024 + dc * 128:(k % 4) * 1024 + (dc + 1) * 128],
                             lambda k: ybT[:, k * 256:(k + 1) * 256], [ybTB], lambda k: (wb0 if k < 4 else wb1))
                  stage("C_inproj")
                  kvq = list(range(i))
                  kv_next = {"n": 0}

                  def kv_issue():
                      n = kv_next["n"]
                      if n >= len(kvq):
                          return
                      j = kvq[n]
                      slot = kv_state.setdefault("cnt", 0) % NKV
                      kv_state["cnt"] += 1
                      kv_state[("slot", i, j)] = slot
                      T.dma("pool", "kvl%d" % slot, kvK[slot][:], kcs[l * NB + j], reads=[kvsB[l * NB + j]], writes=[kvB[slot]])
                      T.dma("pool", "kvl%d" % slot, kvV[slot][:], vcs[l * NB + j], reads=[kvsB[l * NB + j]], writes=[kvB[slot]])
                      kv_next["n"] += 1
                  for _ in range(NKV - 1):
                      kv_issue()
                  wslot, wbuf = w_get(l, i, 12)
                  specs = []
                  for c in range(4):
                      def ev(pa, pb, c=c):
                          for hf in range(2):
                              hh = 2 * c + hf
                              ACT(lambda: nc.scalar.activation(out=QTc[:, hh * 256:(hh + 1) * 256], in_=pa[hf * 64:(hf + 1) * 64, :],
                                                               func=AF.Identity, scale=0.125), pb, [QTcB])
                      specs.append((wslot, wbuf, c * 1024, 0, 128, ev))
                  fm_run(specs)
                  wslot, wbuf = w_get(l, i, 13)
                  specs = []
                  for c in range(4):
                      def ev(pa, pb, c=c):
                          for hf in range(2):
                              hh = 2 * c + hf
                              ACT(lambda: nc.scalar.activation(out=KTcur[:, hh * 256:(hh + 1) * 256], in_=pa[hf * 64:(hf + 1) * 64, :],
                                                               func=AF.Identity, accum_out=ksum[:, hh:hh + 1]), pb, [KTcurB, ksumB])
                      specs.append((wslot, wbuf, c * 1024, 0, 128, ev))
                  fm_run(specs)
                  wslot, wbuf = w_get(l, i, 14)
                  wv = wslot[:].rearrange("p (c k j) -> p c k j", c=4, k=8)
                  for tl in range(2):
                      pg, pgb = gfull()
                      for ko in range(8):
                          PE(lambda: nc.tensor.matmul(pg.rearrange("p (c j) -> p c j", c=4), lhsT=hT3[:, ko, tl * 128:(tl + 1) * 128],
                                                      rhs=wv[:, :, ko, :], start=(ko == 0), stop=(ko == 7)), [wbuf, hTB], pgb)
                      DVE(lambda: nc.vector.tensor_copy(
                          out=Vcur[:, tl * 520:(tl + 1) * 520].rearrange("p (h e) -> p h e", h=8)[:, :, 0:64],
                          in_=pg.rearrange("p (h e) -> p h e", h=8)), pgb, [VcurB])
                  wslot, wbuf = w_get(l, i, 15)
                  fm_run([(wslot, wbuf, (hh // 2) * 1024, (hh % 2) * 64, 64, silu2_evac(sz2c[:, hh * 256:(hh + 1) * 256], sz2cB, 64)) for hh in range(8)])
                  gate_chunks(l, i, 16)
                  if i + 1 < NB:
                      T.dma("pool", "kvst", kcs[l * NB + i], KTcur[:], reads=[KTcurB], writes=[kvsB[l * NB + i]])
                      T.dma("pool", "kvst", vcs[l * NB + i], Vcur[:], reads=[VcurB], writes=[kvsB[l * NB + i]])
                  QTc3 = QTc[:].rearrange("p (c t) -> p c t", c=8)
                  kmT3 = kmT[:].rearrange("p (c j) -> p c j", c=8)
                  selbT3 = selbT[:].rearrange("p (c t) -> p c t", c=8)
                  stage("C_sel")
                  if i >= 1:
                      DVE(lambda: nc.vector.memset(pmask[:].rearrange("p (h j) -> p h j", h=8)[:, :, i - 1:i], 0.0), [], [pmaskB])
                      for tl in range(2):
                          pg, pgb = ghalf()
                          for h in range(8):
                              PE(lambda: nc.tensor.matmul(pg[:, h * 32:(h + 1) * 32], lhsT=QTc3[:, h, tl * 128:(tl + 1) * 128],
                                                          rhs=kmT3[:, h, :], start=True, stop=True), [QTcB, kmTB], pgb)
                          stage("sel_a")
                          DVE(lambda: nc.vector.tensor_tensor(out=gm[:], in0=pg, in1=pmask[:], op=ALU.add), pgb + [pmaskB], [gmB])
                          stage("sel_b")
                          for h in range(8):
                              DVE(lambda: nc.vector.max(out=top8[:, h * 8:(h + 1) * 8], in_=gm[:, h * 32:(h + 1) * 32]), [gmB], [top8B])
                          stage("sel_c")
                          DVE(lambda: nc.vector.tensor_scalar(out=thr[:].rearrange("p (h o) -> p h o", o=1),
                                                              in0=top8[:].rearrange("p (h e) -> p h e", e=8)[:, :, 2:3],
                                                              scalar1=-1e29, scalar2=None, op0=ALU.max), [top8B], [thrB])
                          stage("sel_d")
                          for h in range(8):
                              DVE(lambda: nc.vector.tensor_scalar(out=gm[:, h * 32:(h + 1) * 32], in0=gm[:, h * 32:(h + 1) * 32],
                                                                  scalar1=thr[:, h:h + 1], scalar2=None, op0=ALU.is_ge), [gmB, thrB], [gmB])
                          DVE(lambda: nc.vector.tensor_scalar(out=selm[:], in0=gm[:], scalar1=1.0, scalar2=None, op0=ALU.subtract), [gmB], [selmB])
                          stage("sel_e")
                          pg2, pgb2 = gfull()
                          pg16 = pg2.bitcast(BF16)
                          for h in range(8):
                              PE(lambda: nc.tensor.transpose(pg16[0:32, h * 128:(h + 1) * 128], selm[:, h * 32:(h + 1) * 32], identb),
                                 [selmB, cstbB], pgb2)
                          stage("sel_f")
                          DVE(lambda: nc.vector.tensor_copy(out=selbT3[:, :, tl * 128:(tl + 1) * 128],
                                                            in_=pg16[0:32, :].rearrange("p (c t) -> p c t", c=8)), pgb2, [selbTB])
                  DVE(lambda: nc.vector.tensor_scalar(out=kmT3[:, :, i:i + 1], in0=ksum[:].rearrange("p (c o) -> p c o", o=1),
                                                      scalar1=1.0 / TB, scalar2=None, op0=ALU.mult), [ksumB], [kmTB])
                  stage("C_attn")
                  gst["wide"] = False
                  its = [(j, h) for j in range(i + 1) for h in range(8)]

                  def src_of(j):
                      if j == i:
                          return KTcur, Vcur, [KTcurB, VcurB]
                      slot = kv_state[("slot", i, j)]
                      return kvK[slot], kvV[slot], [kvB[slot]]

                  def emit_qk(n):
                      j, h = its[n]
                      own = (j == i)
                      Kt, Vt, kvb = src_of(j)
                      Kt3 = Kt[:].rearrange("p (c t) -> p c t", c=8)
                      kps, kpt = n % 4, n % 5
                      psb_t, psb_b = PSR[kps]
                      for kt in range(2):
                          o = psb_t[:, kt * 256:(kt + 1) * 256]
                          PE(lambda: nc.tensor.matmul(o, lhsT=Kt3[:, h, kt * 128:(kt + 1) * 128], rhs=QTc3[:, h, :],
                                                      start=True, stop=False), kvb + [QTcB], [psb_b])
                          if own:
                              PE(lambda: nc.tensor.matmul(o, lhsT=identb, rhs=cstb[:, CB_CAUS + kt * 256: CB_CAUS + (kt + 1) * 256],
                                                          start=False, stop=True), [cstbB], [psb_b])
                          else:
                              PE(lambda: nc.tensor.matmul(o, lhsT=cstb[0:32, CB_OH + j * 128: CB_OH + (j + 1) * 128],
                                                          rhs=selbT3[:, h, :], start=False, stop=True), [cstbB, selbTB], [psb_b])
                      ACT(lambda: nc.scalar.activation(out=PT[kpt][:], in_=psb_t[:, :], func=AF.Exp), [psb_b], [PTB[kpt]])

                  def emit_pv(n):
                      j, h = its[n]
                      own = (j == i)
                      Kt, Vt, kvb = src_of(j)
                      kpt = n % 5
                      for kt in range(2):
                          PE(lambda: nc.tensor.matmul(PO[h // 2][0:65, (h % 2) * 256:(h % 2 + 1) * 256],
                                                      lhsT=Vt[:, kt * 520 + h * 65: kt * 520 + (h + 1) * 65],
                                                      rhs=PT[kpt][:, kt * 256:(kt + 1) * 256],
                                                      start=(j == 0 and kt == 0 and h % 2 == 0), stop=(own and kt == 1 and h % 2 == 1)),
                             kvb + [PTB[kpt]], [POB[h // 2]])
                      if h == 7 and not own:
                          kv_issue()
                  PSR = [(PS[0], PSB[0]), (PS[1], PSB[1]), (G[0], GB[0]), (G[1], GB[1])]
                  LA = 3
                  for n in range(min(LA, len(its))):
                      emit_qk(n)
                  for n in range(len(its)):
                      if n + LA < len(its):
                          emit_qk(n + LA)
                      emit_pv(n)
                  stage("C_norm")
                  ycT3 = ycT[:].rearrange("p (h t) -> p h t", h=8)
                  sz2c3 = sz2c[:].rearrange("p (h t) -> p h t", h=8)
                  for hp in range(4):
                      normalize(PO[hp][0:65, :], [POB[hp]], 512, ycT3[:, hp * 2:(hp + 1) * 2, :], ycTB,
                                sz2c3[:, hp * 2:(hp + 1) * 2, :], sz2cB, hp % 2)
                  gst["wide"] = True
                  wsl0, wb0 = w_get(l, i, 18)
                  wsl1, wb1 = w_get(l, i, 19)
                  proj_merge(False, 8,
                             lambda k, dc: (wsl0 if k < 4 else wsl1)[0:64, (k % 4) * 1024 + dc * 128:(k % 4) * 1024 + (dc + 1) * 128],
                             lambda k: ycT[:, k * 256:(k + 1) * 256], [ycTB], lambda k: (wb0 if k < 4 else wb1))
                  stage("out")
                  DVE(lambda: nc.vector.tensor_copy(out=mT[:], in_=acc[:]), [accB], [mTB])
                  wsl0, wb0 = w_get(l, i, 20)
                  wsl1, wb1 = w_get(l, i, 21)
                  for tl in range(2):
                      s = xslot(i, tl)
                      t = i * 2 + tl
                      for cg in range(2):
                          pg, pgb = gfull()
                          for ko in range(8):
                              wsl, wb = (wsl0, wb0) if ko < 4 else (wsl1, wb1)
                              PE(lambda: nc.tensor.matmul(pg, lhsT=mT3[:, ko, tl * 128:(tl + 1) * 128],
                                                          rhs=wsl[:, (ko % 4) * 1024 + cg * 512:(ko % 4) * 1024 + (cg + 1) * 512],
                                                          start=(ko == 0), stop=(ko == 7)), [mTB, wb], pgb)
                          DVE(lambda: nc.vector.scalar_tensor_tensor(out=xt[s][:, cg * 512:(cg + 1) * 512], in0=pg, scalar=0.5,
                                                                     in1=xt[s][:, cg * 512:(cg + 1) * 512], op0=ALU.mult, op1=ALU.add),
                              pgb + [xtB[s]], [xtB[s]])
                      if l + 1 < NL:
                          T.dma("pool", "xst%d" % s, x1[t * 128:(t + 1) * 128, :], xt[s][:], reads=[xtB[s]], writes=[x1B[t]])
                  if l + 1 == NL:
                      for tl in range(2):
                          s = xslot(i, tl)
                          ACT(lambda: nc.scalar.activation(out=sqj[:], in_=xt[s][:], func=AF.Square, accum_out=ss2[:, tl:tl + 1]),
                              [xtB[s]], [sqjB, ss2B])
                      DVE(lambda: nc.vector.tensor_scalar(out=rs2[:], in0=ss2[:], scalar1=1.0 / D, scalar2=EPS, op0=ALU.mult, op1=ALU.add),
                          [ss2B], [rs2B])
                      ACT(lambda: nc.scalar.activation(out=rs2[:], in_=rs2[:], func=AF.Sqrt), [rs2B], [rs2B])
                      DVE(lambda: nc.vector.reciprocal(out=rs2[:], in_=rs2[:]), [rs2B], [rs2B])
                      for tl in range(2):
                          s = xslot(i, tl)
                          t = i * 2 + tl
                          DVE(lambda: nc.vector.scalar_tensor_tensor(out=acc[:, tl * D:(tl + 1) * D], in0=xt[s][:], scalar=rs2[:, tl:tl + 1], in1=fgbc[:],
                                                                     op0=ALU.mult, op1=ALU.mult), [xtB[s], rs2B, fgbcB], [accB])
                          T.dma("pool", "osd%d" % tl, out[t * 128:(t + 1) * 128, :], acc[:, tl * D:(tl + 1) * D], reads=[accB], writes=[outB])
        except _Stop:
            pass
        T.barrier()
        T.close()
    return nc


def _consts():
    cf = np.zeros((128, CF_N), np.float32)
    cf[:, CF_ID:CF_ID + 128] = np.eye(128, dtype=np.float32)
    tp = np.arange(128)[:, None]
    t = np.arange(128)[None, :]
    for g, w in enumerate((2, 4, 8, 16)):
        cur = ((tp <= t) & (tp > t - w)).astype(np.float32) / w - np.eye(128, dtype=np.float32)
        prv = ((tp - 128 > t - w)).astype(np.float32) / w
        cnt = np.minimum(t + 1, w).astype(np.float32)
        fst = ((tp <= t) & (tp > t - w)).astype(np.float32) / cnt - np.eye(128, dtype=np.float32)
        for k, m in enumerate((cur, prv, fst)):
            c0 = CF_BAND + (g * 3 + k) * 128
            cf[:, c0:c0 + 128] = m
    cf[0, CF_E64 + 64] = 1.0
    cf[:, CF_ONES:CF_ONES + 64] = 1.0
    cf[64, CF_S65:CF_S65 + 64] = 1.0
    cb = np.zeros((128, CB_N), np.float32)
    cb[:, CB_ID:CB_ID + 128] = np.eye(128)
    k = np.arange(128)[:, None]
    q = np.arange(256)[None, :]
    for kt in range(2):
        cb[:, CB_CAUS + kt * 256: CB_CAUS + (kt + 1) * 256] = np.where(kt * 128 + k <= q, 0.0, NEG)
    q1 = np.arange(128)[None, :]
    own = np.where(k <= q1, 0.0, NEG)
    prv = np.where(k > q1, 0.0, NEG)
    for c in range(4):
        cb[:, CB_OWN4 + c * 128: CB_OWN4 + (c + 1) * 128] = own
        cb[:, CB_PRV4 + c * 128: CB_PRV4 + (c + 1) * 128] = prv
    for p in range(32):
        j = p
        cb[p, CB_OH + j * 128: CB_OH + (j + 1) * 128] = -NEG
    return cf, cb.astype(ml_dtypes.bfloat16)


def _inproj_unit(w, cols):
    cols = np.asarray(cols)
    sub = np.zeros((1024, 512), np.float32)
    ok = cols >= 0
    sub[:, ok] = w[:, cols[ok]]
    u = sub.reshape(8, 128, 4, 128).transpose(1, 2, 0, 3)
    return np.ascontiguousarray(u).reshape(128, UE)


def _layer_units(w_in, wpa, wpb, wpc, wout):
    r = np.arange
    A_X, A_Z, B_Q, B_K, B_V, B_Z, C_Q, C_K, C_V, C_Z, GT = 0, 512, 1024, 1536, 1664, 1792, 2304, 2816, 3328, 3840, 4352
    units = []
    units.append(_inproj_unit(w_in, A_X + r(512)))
    units.append(_inproj_unit(w_in, A_Z + r(512)))
    units.append(_inproj_unit(w_in, GT + r(512)))
    units.append(_inproj_unit(w_in, GT + 512 + r(512)))
    units.append(np.ascontiguousarray(wpa.reshape(4, 128, 1024).transpose(1, 0, 2)).reshape(128, UE))
    units.append(_inproj_unit(w_in, B_Q + r(512)))
    units.append(_inproj_unit(w_in, np.concatenate([B_K + r(128), B_V + r(128), -np.ones(256, np.int64)])))
    units.append(_inproj_unit(w_in, B_Z + r(512)))
    units.append(_inproj_unit(w_in, GT + 1024 + r(512)))
    units.append(_inproj_unit(w_in, GT + 1536 + r(512)))
    for wp in (wpb,):
        for half in range(2):
            u = np.zeros((128, UE), np.float32)
            u[0:64] = wp.reshape(8, 64, 1024)[half * 4:(half + 1) * 4].transpose(1, 0, 2).reshape(64, UE)
            units.append(u)
    units.append(_inproj_unit(w_in, C_Q + r(512)))
    units.append(_inproj_unit(w_in, C_K + r(512)))
    units.append(_inproj_unit(w_in, C_V + r(512)))
    units.append(_inproj_unit(w_in, C_Z + r(512)))
    units.append(_inproj_unit(w_in, GT + 2048 + r(512)))
    units.append(_inproj_unit(w_in, GT + 2560 + r(512)))
    for half in range(2):
        u = np.zeros((128, UE), np.float32)
        u[0:64] = wpc.reshape(8, 64, 1024)[half * 4:(half + 1) * 4].transpose(1, 0, 2).reshape(64, UE)
        units.append(u)
    wo = wout.reshape(8, 128, 1024).transpose(1, 0, 2)
    units.append(np.ascontiguousarray(wo[:, 0:4]).reshape(128, UE))
    units.append(np.ascontiguousarray(wo[:, 4:8]).reshape(128, UE))
    assert len(units) == NU
    return np.stack(units)


def prep_shared(norm_g, w_in, pool_w, pool_scale, sink_logits, w_proj_a, w_proj_b, w_proj_c, w_out, final_norm_g):
    NL = w_in.shape[0]
    f = lambda a: np.asarray(a, np.float32)
    norm_g, w_in, pool_w, pool_scale, sink_logits = f(norm_g), f(w_in), f(pool_w), f(pool_scale), f(sink_logits)
    w_proj_a, w_proj_b, w_proj_c, w_out, final_norm_g = f(w_proj_a), f(w_proj_b), f(w_proj_c), f(w_out), f(final_norm_g)
    wall = np.concatenate([_layer_units(w_in[l], w_proj_a[l], w_proj_b[l], w_proj_c[l], w_out[l]) for l in range(NL)])
    gbc = np.stack([np.broadcast_to(norm_g[l], (128, D)) for l in range(NL)] + [np.broadcast_to(final_norm_g, (128, D))])
    poolw = np.stack([np.ascontiguousarray(pool_w[l].transpose(1, 0, 2)).reshape(128, 512) for l in range(NL)])
    pscale = np.stack([np.ascontiguousarray(pool_scale[l].reshape(4, 128).T) for l in range(NL)])
    sink = np.stack([np.repeat(sink_logits[l], 128).reshape(1, 1024) for l in range(NL)])
    cf, cb = _consts()
    return {"wall": np.ascontiguousarray(wall, np.float32), "gbc": np.ascontiguousarray(gbc, np.float32),
            "poolw": np.ascontiguousarray(poolw, np.float32), "pscale": np.ascontiguousarray(pscale, np.float32),
            "sink": np.ascontiguousarray(sink, np.float32), "cstf": cf, "cstb": cb}


def run(x, shared, n_cores):
    B, S, _ = x.shape
    NL = shared["wall"].shape[0] // NU
    nc = build_nc(S, NL)
    in_maps = []
    for c in range(n_cores):
        m = dict(shared)
        m["x"] = np.ascontiguousarray(x[c % B], np.float32)
        in_maps.append(m)
    res = run_bass_kernel_spmd(nc, in_maps, core_ids=list(range(n_cores)))
    return np.stack([np.asarray(res.results[c]["out"]) for c in range(B)]).astype(np.float32)


def kernel(x, norm_g, w_in, pool_w, pool_scale, sink_logits, w_proj_a, w_proj_b, w_proj_c, w_out, final_norm_g):
    x = np.asarray(x, np.float32)
    shared = prep_shared(norm_g, w_in, pool_w, pool_scale, sink_logits, w_proj_a, w_proj_b, w_proj_c, w_out, final_norm_g)
    return run(x, shared, x.shape[0])
```
